# Optimizing a Trainium2 kernel written in Bass

```python
import jax, jax.numpy as jnp
from jax import lax
import numpy as np

D_MODEL = 2048
BATCH = 8
SEQ = 4096
DEPTH = 4

N_MIXERS = 2
N_RET_LAYERS = (DEPTH + 1) // 2
N_SWA_LAYERS = DEPTH // 2

RET_HEADS = 8
RET_QK_DIM = D_MODEL // RET_HEADS
RET_V_DIM = 2 * RET_QK_DIM
RET_VALUE_WIDTH = RET_HEADS * RET_V_DIM
RET_CHUNK = 128
RET_IN = 2 * RET_HEADS * RET_QK_DIM + 2 * RET_VALUE_WIDTH

SWA_HEAD_DIM = 64
SWA_HEADS = D_MODEL // SWA_HEAD_DIM
SWA_KV_HEADS = 4
SWA_GROUP = SWA_HEADS // SWA_KV_HEADS
SWA_WINDOW = 128
SWA_Q_WIDTH = SWA_HEADS * SWA_HEAD_DIM
SWA_KV_WIDTH = SWA_KV_HEADS * SWA_HEAD_DIM
SWA_IN = SWA_Q_WIDTH + 2 * SWA_KV_WIDTH

FFN_HIDDEN = ((-(-8 * D_MODEL // 3)) + 255) // 256 * 256

PLE_DIM = 256

NORM_EPS = 1e-6

kernel_name = "hybrid_retention_swa_sink_trunk"


def rmsnorm(x, gain):
    xf = x.astype(jnp.float32)
    inv = lax.rsqrt(jnp.mean(xf * xf, axis=-1, keepdims=True) + NORM_EPS)
    return (xf * inv * gain.astype(jnp.float32)).astype(x.dtype)


def retention_mixer(h, w_in, gn_gain, w_out):
    B, S, _ = h.shape
    H, DK, DV, C = RET_HEADS, RET_QK_DIM, RET_V_DIM, RET_CHUNK
    NC = S // C
    proj = h @ w_in
    q, k, v, g = jnp.split(proj, [H * DK, 2 * H * DK, 2 * H * DK + RET_VALUE_WIDTH], axis=-1)
    k = k * (DK ** -0.5)

    def to_chunks(t, d):
        return t.reshape(B, NC, C, H, d).transpose(1, 0, 3, 2, 4)

    qc, kc, vc = to_chunks(q, DK), to_chunks(k, DK), to_chunks(v, DV)

    log_g = jnp.log1p(-jnp.exp2(-5.0 - jnp.arange(H, dtype=jnp.float32)))
    pos = jnp.arange(C, dtype=jnp.float32)
    diff = pos[:, None] - pos[None, :]
    causal = diff >= 0
    decay_intra = jnp.where(causal, jnp.exp(log_g[:, None, None] * jnp.where(causal, diff, 0.0)), 0.0)
    q_decay = jnp.exp(log_g[:, None] * (pos + 1.0))[:, :, None]
    k_decay = jnp.exp(log_g[:, None] * (C - 1.0 - pos))[:, :, None]
    chunk_decay = jnp.exp(log_g * C)[:, None, None]
    dt = h.dtype
    decay_intra, q_decay, k_decay, chunk_decay = (a.astype(dt) for a in (decay_intra, q_decay, k_decay, chunk_decay))

    def step(state, inp):
        qb, kb, vb = inp
        scores = jnp.einsum('bhcd,bhmd->bhcm', qb, kb) * decay_intra
        o = jnp.einsum('bhcm,bhme->bhce', scores, vb) + jnp.einsum('bhcd,bhde->bhce', qb * q_decay, state)
        state = state * chunk_decay + jnp.einsum('bhcd,bhce->bhde', kb * k_decay, vb)
        return state, o

    state0 = jnp.zeros((B, H, DK, DV), dtype=dt)
    _, y = lax.scan(step, state0, (qc, kc, vc))
    y = y.transpose(1, 0, 3, 2, 4).reshape(B, S, H, DV)

    yf = y.astype(jnp.float32)
    mu = jnp.mean(yf, axis=-1, keepdims=True)
    var = jnp.mean(jnp.square(yf - mu), axis=-1, keepdims=True)
    yn = ((yf - mu) * lax.rsqrt(var + NORM_EPS) * gn_gain.astype(jnp.float32)).astype(dt)
    yn = yn.reshape(B, S, RET_VALUE_WIDTH)
    return (jax.nn.silu(g) * yn) @ w_out


def swa_mixer(h, w_in, sinks, w_out):
    B, S, _ = h.shape
    W, HKV, G, HD = SWA_WINDOW, SWA_KV_HEADS, SWA_GROUP, SWA_HEAD_DIM
    NB = S // W
    proj = h @ w_in
    q, k, v = jnp.split(proj, [SWA_Q_WIDTH, SWA_Q_WIDTH + SWA_KV_WIDTH], axis=-1)
    q = (q * (HD ** -0.5)).reshape(B, NB, W, HKV, G, HD)
    k = k.reshape(B, NB, W, HKV, HD)
    v = v.reshape(B, NB, W, HKV, HD)

    def band(t):
        prev = jnp.concatenate([jnp.zeros_like(t[:, :1]), t[:, :-1]], axis=1)
        return jnp.concatenate([prev, t], axis=2)

    kb_all = jnp.moveaxis(band(k), 1, 0)
    vb_all = jnp.moveaxis(band(v), 1, 0)
    qb_all = jnp.moveaxis(q, 1, 0)

    qi = jnp.arange(W)[:, None]
    kj = jnp.arange(2 * W)[None, :]
    rel = qi - kj + W
    in_window = (rel >= 0) & (rel < W)
    slopes = jnp.exp2(-8.0 * (jnp.arange(SWA_HEADS, dtype=jnp.float32) + 1.0) / SWA_HEADS).reshape(HKV, G)
    alibi = -slopes[:, :, None, None] * rel.astype(jnp.float32)
    sink = sinks.astype(jnp.float32).reshape(HKV, G)[None, :, :, None, None]

    def block_attn(args):
        qb, kb, vb, blk = args
        s = jnp.einsum('bqkgd,bskd->bkgqs', qb, kb).astype(jnp.float32) + alibi
        valid = in_window & ((kj >= W) | (blk > 0))
        s = jnp.where(valid, s, -jnp.inf)
        m = jnp.maximum(jnp.max(s, axis=-1, keepdims=True), sink)
        e = jnp.exp(s - m)
        probs = e / (jnp.sum(e, axis=-1, keepdims=True) + jnp.exp(sink - m))
        return jnp.einsum('bkgqs,bskd->bqkgd', probs.astype(vb.dtype), vb)

    o = lax.map(block_attn, (qb_all, kb_all, vb_all, jnp.arange(NB)))
    o = jnp.moveaxis(o, 0, 1).reshape(B, S, SWA_Q_WIDTH)
    return o @ w_out


def swiglu(h, w_in, w_out):
    gate, up = jnp.split(h @ w_in, 2, axis=-1)
    return (jax.nn.silu(gate) * up) @ w_out


def setup_inputs(seed: int = 0) -> dict:
    key = jax.random.key(seed)
    ks = jax.random.split(key, 16)
    f32 = jnp.float32

    def w(k, shape, fan_in, scale=1.0):
        return jax.random.normal(k, shape, f32) * (scale * fan_in ** -0.5)

    def gain(k, shape):
        return 1.0 + 0.02 * jax.random.normal(k, shape, f32)

    res_scale = (2.0 * DEPTH) ** -0.5
    return {
        "x": jax.random.normal(ks[0], (BATCH, SEQ, D_MODEL), f32),
        "p": jax.random.normal(ks[1], (DEPTH, BATCH, SEQ, PLE_DIM), f32),
        "norm_mix": gain(ks[2], (DEPTH, D_MODEL)),
        "norm_ffn": gain(ks[3], (DEPTH, D_MODEL)),
        "norm_ple": gain(ks[4], (DEPTH, D_MODEL)),
        "norm_final": gain(ks[5], (D_MODEL,)),
        "ret_w_in": w(ks[6], (N_RET_LAYERS, D_MODEL, RET_IN), D_MODEL),
        "ret_gn": gain(ks[7], (N_RET_LAYERS, RET_HEADS, RET_V_DIM)),
        "ret_w_out": w(ks[8], (N_RET_LAYERS, RET_VALUE_WIDTH, D_MODEL), RET_VALUE_WIDTH, res_scale),
        "swa_w_in": w(ks[9], (N_SWA_LAYERS, D_MODEL, SWA_IN), D_MODEL),
        "swa_sinks": 0.5 * jax.random.normal(ks[10], (N_SWA_LAYERS, SWA_HEADS), f32),
        "swa_w_out": w(ks[11], (N_SWA_LAYERS, SWA_Q_WIDTH, D_MODEL), SWA_Q_WIDTH, res_scale),
        "ffn_w_in": w(ks[12], (DEPTH, D_MODEL, 2 * FFN_HIDDEN), D_MODEL),
        "ffn_w_out": w(ks[13], (DEPTH, FFN_HIDDEN, D_MODEL), FFN_HIDDEN, res_scale),
        "ple_w_proj": w(ks[14], (DEPTH, PLE_DIM, D_MODEL), PLE_DIM, res_scale),
        "ple_w_gate": w(ks[15], (DEPTH, D_MODEL, D_MODEL), D_MODEL),
    }


def reference(x, p, norm_mix, norm_ffn, norm_ple, norm_final, ret_w_in, ret_gn, ret_w_out,
              swa_w_in, swa_sinks, swa_w_out, ffn_w_in, ffn_w_out, ple_w_proj, ple_w_gate):
    for i in range(DEPTH):
        h = rmsnorm(x, norm_mix[i])
        j = i // N_MIXERS
        if i % N_MIXERS == 0:
            x = x + retention_mixer(h, ret_w_in[j], ret_gn[j], ret_w_out[j])
        else:
            x = x + swa_mixer(h, swa_w_in[j], swa_sinks[j], swa_w_out[j])
        x = x + swiglu(rmsnorm(x, norm_ffn[i]), ffn_w_in[i], ffn_w_out[i])
        gate = jax.nn.sigmoid(rmsnorm(x, norm_ple[i]) @ ple_w_gate[i])
        x = x + gate * (p[i] @ ple_w_proj[i])
    return rmsnorm(x, norm_final)
```

```python
import math
from contextlib import ExitStack
from functools import partial

import numpy as np
import ml_dtypes

import concourse.bass as bass
import concourse.mybir as mybir
from concourse.bass_utils import run_bass_kernel_spmd

F32 = mybir.dt.float32
BF16 = mybir.dt.bfloat16
ALU = mybir.AluOpType
AF = mybir.ActivationFunctionType

D = 2048
SEQ = 4096
DEPTH = 4
TT = 512
NTB = TT // 128
DC = D // 128
RH, RDK, RDV = 8, 256, 512
RET_IN = 12288
RVW = 4096
SH, SKV, SHD = 32, 4, 64
SWA_IN = 2560
FH = 5632
PLE = 256
EPS = 1e-6
NWS = 3
WSLOT = 8192


class Op:
    __slots__ = ("eng", "fn", "deps", "sem", "inc", "need", "cnt", "isdma")


class Prog:
    def __init__(self):
        self.ops = []
        self.last_w = {}
        self.readers = {}

    def add(self, eng, fn, reads=(), writes=(), dma_sem=None):
        op = Op()
        op.eng = eng
        op.fn = fn
        op.isdma = dma_sem is not None
        op.sem = dma_sem if op.isdma else eng
        op.inc = 16 if op.isdma else 1
        op.need = op.isdma
        op.cnt = 0
        deps = []
        lw, rd = self.last_w, self.readers
        for r in reads:
            w = lw.get(r)
            if w is not None:
                deps.append(w)
        for r in writes:
            w = lw.get(r)
            if w is not None:
                deps.append(w)
            rs = rd.get(r)
            if rs:
                deps.extend(rs)
        for r in reads:
            rd.setdefault(r, []).append(op)
        for r in writes:
            lw[r] = op
            rd[r] = []
        dd = []
        seen = set()
        for p in deps:
            if id(p) in seen:
                continue
            seen.add(id(p))
            if (not p.isdma) and (not op.isdma) and p.eng == "pe" and eng == "pe":
                continue
            p.need = True
            dd.append(p)
        op.deps = dd
        self.ops.append(op)
        return op

    def finalize(self):
        counts = {}
        for op in self.ops:
            if op.need:
                counts[op.sem] = counts.get(op.sem, 0) + op.inc
                op.cnt = counts[op.sem]
        return counts

    def emit(self, eng_name, eng, sems):
        waited = {}
        for op in self.ops:
            if op.eng != eng_name:
                continue
            need = {}
            for p in op.deps:
                if p.cnt > need.get(p.sem, 0):
                    need[p.sem] = p.cnt
            for key, cnt in need.items():
                if waited.get(key, 0) >= cnt:
                    continue
                waited[key] = cnt
                eng.wait_ge(sems[key], cnt)
            ins = op.fn(eng)
            if op.need:
                ins.then_inc(sems[op.sem], op.inc)


def _consts():
    bf = ml_dtypes.bfloat16
    c = {}
    c["c_ident"] = np.eye(128, dtype=np.float32).astype(bf)
    c["c_identf"] = np.eye(128, dtype=np.float32)
    c["c_ones"] = np.ones((128, 128), dtype=np.float32).astype(bf)
    m = np.arange(128)[:, None]
    q = np.arange(128)[None, :]
    c["c_causal"] = (q >= m).astype(np.float32).astype(bf)
    log_g = np.log1p(-np.exp2(-5.0 - np.arange(RH, dtype=np.float64)))
    pos = np.arange(128, dtype=np.float64)
    qdec = np.exp(log_g[:, None] * (pos[None, :] + 1.0))
    c["c_qdec"] = np.broadcast_to(qdec.reshape(1, RH * 128), (128, RH * 128)).astype(np.float32).copy()
    kdec = (RDK ** -0.5) * np.exp(log_g[None, :] * (127.0 - pos[:, None]))
    c["c_kdec"] = kdec.astype(np.float32).copy()
    slopes = np.exp2(-8.0 * (np.arange(SH, dtype=np.float64) + 1.0) / SH)
    s = np.arange(128)[:, None]
    qq = np.arange(128)[None, :]
    tbl = np.zeros((128, SH, 2, 128), dtype=np.float64)
    rel_prev = (qq - s + 128).astype(np.float64)
    rel_cur = (qq - s).astype(np.float64)
    for h in range(SH):
        tbl[:, h, 0, :] = np.where(qq < s, np.exp(-slopes[h] * rel_prev), 0.0)
        tbl[:, h, 1, :] = np.where(qq >= s, np.exp(-slopes[h] * rel_cur), 0.0)
    c["c_swatbl"] = tbl.reshape(128, SH * 2 * 128).astype(np.float32).astype(bf)
    return c


RET_GAMMA = [1.0 - 2.0 ** (-5.0 - h) for h in range(RH)]
RET_CHUNK_DECAY = [float(np.float32(np.exp(np.float64(128.0) * np.log1p(-np.exp2(-5.0 - h))))) for h in range(RH)]
RET_INV_CD = [float(np.exp(-np.float64(128.0) * np.log1p(-np.exp2(-5.0 - h)))) for h in range(RH)]


def build(NT=SEQ // TT, NL=DEPTH):
    nc = bass.Bass("TRN2", target_bir_lowering=False)
    P = Prog()
    es = ExitStack()

    def din(name, shape, dt=F32):
        return nc.dram_tensor(name, list(shape), dt, kind="ExternalInput").ap()

    x_d = din("x", [SEQ, D])
    p_d = din("p", [DEPTH, SEQ, PLE])
    nmix_d = din("norm_mix", [DEPTH, D])
    nffn_d = din("norm_ffn", [DEPTH, D])
    nple_d = din("norm_ple", [DEPTH, D])
    nfin_d = din("norm_final", [1, D])
    rwin_d = din("ret_w_in", [2, D, RET_IN])
    rgn_d = din("ret_gn", [2, RH * RDV])
    rwout_d = din("ret_w_out", [2, RVW, D])
    swin_d = din("swa_w_in", [2, D, SWA_IN])
    ssink_d = din("swa_sinks", [1, 2 * SH])
    swout_d = din("swa_w_out", [2, D, D])
    fwin_d = din("ffn_w_in", [DEPTH, D, 2 * FH])
    fwout_d = din("ffn_w_out", [DEPTH, FH, D])
    pproj_d = din("ple_w_proj", [DEPTH, PLE, D])
    pgate_d = din("ple_w_gate", [DEPTH, D, D])
    cid_d = din("c_ident", [128, 128], BF16)
    cidf_d = din("c_identf", [128, 128])
    cones_d = din("c_ones", [128, 128], BF16)
    ccaus_d = din("c_causal", [128, 128], BF16)
    cqdec_d = din("c_qdec", [128, RH * 128])
    ckdec_d = din("c_kdec", [128, RH])
    ctbl_d = din("c_swatbl", [128, SH * 2 * 128], BF16)
    out_d = nc.dram_tensor("out", [SEQ, D], F32, kind="ExternalOutput").ap()
    st_d = nc.dram_tensor("state_scratch", [2 * RH * 128, 2 * RDV], F32, kind="Internal").ap()
    WCAP = 1000000
    wbf_ds = [nc.dram_tensor("wbf_scratch%d" % i, [128, WCAP], BF16, kind="Internal").ap() for i in range(2)]
    uoff = []
    walloc = {"i": 0, "off": 0}

    def sb(name, shape, dt):
        return es.enter_context(nc.sbuf_tensor(name, list(shape), dt))

    xT = sb("xT", [128, DC, TT], F32)
    hT = sb("hT", [128, DC, TT], BF16)
    big = sb("big", [128, 32, TT], BF16)
    wsl = sb("wsl", [128, NWS, WSLOT], BF16)
    mixb = sb("mixb", [128, 23040], BF16)
    mixf = sb("mixf", [128, 2, 2 * RDV], F32)
    gains = sb("gains", [128, 13, DC], F32)
    ident = sb("ident", [128, 128], BF16)
    identf = sb("identf", [128, 128], F32)
    ones = sb("ones", [128, 128], BF16)
    causal = sb("causal", [128, 128], BF16)
    qdec = sb("qdec", [128, RH, 128], F32)
    kdec = sb("kdec", [128, RH], F32)
    esink = sb("esink", [128, 2 * SH], F32)
    sq = sb("sq", [128, 2, TT], BF16)
    rstd = sb("rstd", [128, TT], F32)
    ftmp = sb("ftmp", [128, 2, TT], F32)
    kcar = sb("kcar", [128, 2, SKV * 2 * 128], BF16)
    vcar = sb("vcar", [128, 2, SKV * 65], BF16)
    stat = sb("stat", [128, 4, 16], F32)
    sstat = sb("sstat", [128, 4, 8], F32)
    epsb = sb("epsb", [128, 8], F32)
    ps = es.enter_context(nc.psum_tensor("ps", [128, 8, 512], F32))

    bigf = big[:].rearrange("p a b -> p (a b)").bitcast(F32)
    stage = bigf.rearrange("p (t d) -> p t d", t=NTB)

    off = [0]

    def carve(n, shape_str=None, **kw):
        a = mixb[:, off[0]:off[0] + n]
        off[0] += n
        if shape_str:
            a = a.rearrange(shape_str, **kw)
        return a

    r_qT = [carve(2 * TT, "p (c t) -> p c t", c=2) for _ in range(2)]
    r_kpp = [carve(NTB * RDK, "p (t d) -> p t d", t=NTB) for _ in range(2)]
    r_v = [carve(NTB * RDV, "p (t e) -> p t e", t=NTB) for _ in range(2)]
    r_sg = [carve(NTB * RDV, "p (t e) -> p t e", t=NTB) for _ in range(2)]
    r_kT = [carve(2 * 128, "p (c t) -> p c t", c=2) for _ in range(2)]
    r_sT = [carve(128) for _ in range(2)]
    r_Sbf = [carve(2 * RDV, "p (c e) -> p c e", c=2) for _ in range(2)]
    r_yn = [carve(RDV) for _ in range(2)]
    r_z = [carve(RDV) for _ in range(2)]
    r_gn = carve(RH * RDV, "p (h e) -> p h e", h=RH)
    ret_end = off[0]
    off[0] = 0
    s_tbl = carve(SH * 2 * 128, "p (h k q) -> p h k q", h=SH, k=2)
    s_kpad = carve(NTB * SKV * 2 * 128, "p (t k a d) -> p t k a d", t=NTB, k=SKV, a=2)
    s_kTp = carve(5 * SKV * 2 * 128, "p (t k a s) -> p t k a s", t=5, k=SKV, a=2)
    s_vaug = carve(5 * SKV * 65, "p (t k e) -> p t k e", t=5, k=SKV)
    s_e = [carve(512) for _ in range(4)]
    s_otm = [carve(D) for _ in range(1)]
    swa_end = off[0]
    assert max(ret_end, swa_end) <= 23040, (ret_end, swa_end)

    ctr = {"bank": 0, "ws": 0, "sq": 0, "ft": 0, "alt": 0, "st": 0, "sst": 0}

    def next_bank():
        b = ctr["bank"]
        ctr["bank"] = (b + 1) % 6
        return b

    def alt_engine():
        ctr["alt"] ^= 1
        return "act" if ctr["alt"] else "dve"

    def MM(out, lhsT, rhs, start, stop, reads, writes):
        P.add("pe", lambda e: e.matmul(out, lhsT, rhs, start=start, stop=stop), reads, writes)

    def TR(out, in_, idt, reads, writes):
        P.add("pe", lambda e: e.transpose(out, in_, idt), reads, writes)

    def ACT(out, in_, func, reads, writes, bias=None, scale=None):
        kw = {}
        if bias is not None:
            kw["bias"] = bias
        if scale is not None:
            kw["scale"] = scale
        P.add("act", lambda e: e.activation(out, in_, func, **kw), reads, writes)

    def TT_(eng, out, in0, in1, op, reads, writes):
        P.add(eng, lambda e: e.tensor_tensor(out, in0, in1, op), reads, writes)

    def TS(eng, out, in0, s1, s2, op0, op1, reads, writes):
        if op1 is None:
            P.add(eng, lambda e: e.tensor_scalar(out, in0, s1, None, op0), reads, writes)
        else:
            P.add(eng, lambda e: e.tensor_scalar(out, in0, s1, s2, op0, op1), reads, writes)

    def STT(eng, out, in0, scalar, in1, op0, op1, reads, writes):
        P.add(eng, lambda e: e.scalar_tensor_tensor(out, in0, scalar, in1, op0, op1), reads, writes)

    def COPY(eng, out, in_, reads, writes):
        if eng == "act":
            P.add("act", lambda e: e.copy(out, in_), reads, writes)
        else:
            P.add(eng, lambda e: e.tensor_copy(out, in_), reads, writes)

    def DMA(q, out, in_, sem, reads, writes, slow=False):
        if slow:
            P.add(q, lambda e: e.dma_start(out=out, in_=in_, allow_slow_non_contiguous=True), reads, writes, dma_sem=sem)
        else:
            P.add(q, lambda e: e.dma_start(out=out, in_=in_), reads, writes, dma_sem=sem)

    def evac_copy(out, in_, reads, writes):
        COPY(alt_engine(), out, in_, reads, writes)

    cur = {"t": 0, "u": 0}

    def load_w(parts, KC):
        s = ctr["ws"]
        ctr["ws"] = (s + 1) % NWS
        u = cur["u"]
        cur["u"] += 1
        ntot = sum(pp.shape[1] for pp in parts)
        nel = KC * ntot
        assert nel <= WSLOT, (KC, ntot)
        view = wsl[:, s, 0:nel].rearrange("p (k n) -> p k n", k=KC)
        if cur["t"] == 0:
            if walloc["off"] + nel > WCAP:
                walloc["i"] += 1
                walloc["off"] = 0
            assert walloc["i"] < len(wbf_ds)
            uoff.append((walloc["i"], walloc["off"]))
            walloc["off"] += nel
        wi, wo = uoff[u]
        scr = wbf_ds[wi][:, wo:wo + nel]
        if cur["t"] == 0:
            o = 0
            for pp in parts:
                n = pp.shape[1]
                DMA("pool", view[:, :, o:o + n], pp.rearrange("(k p) n -> p k n", p=128), "w%d" % s,
                    reads=[], writes=[("w", s)])
                o += n
            if NT > 1:
                DMA("sq", scr, wsl[:, s, 0:KC * ntot], "wb%d" % s, reads=[("w", s)], writes=[("wbf", u)])
        else:
            DMA("sq", wsl[:, s, 0:KC * ntot], scr, "v%d" % s, reads=[("wbf", u)], writes=[("w", s)])
        return view, ("w", s)

    def lin_fm(wview, wres, KC, nsl, act_fn, col0=0):
        b = next_bank()
        for kc in range(KC):
            a, ares = act_fn(kc)
            MM(ps[:, b, :], wview[:, kc, col0 + nsl * 128:col0 + (nsl + 1) * 128], a, kc == 0, kc == KC - 1,
               reads=[wres, ares], writes=[("ps", b)])
        return b

    def lin_tm(wview, wres, KC, tb, c0, n, act3, actres_fn):
        b = next_bank()
        for kc in range(KC):
            MM(ps[:, b, 0:n], act3[:, kc, tb * 128:(tb + 1) * 128], wview[:, kc, c0:c0 + n], kc == 0, kc == KC - 1,
               reads=[wres, actres_fn(kc)], writes=[("ps", b)])
        return b

    def hres(kc):
        return ("h", kc)

    def h_act(kc):
        return hT[:, kc, :], ("h", kc)

    def rmsnorm(grow, final=False):
        b = next_bank()
        for c in range(DC):
            i = ctr["sq"]
            ctr["sq"] ^= 1
            ACT(sq[:, i, :], xT[:, c, :], AF.Square, reads=[("x", c)], writes=[("sq", i)])
            MM(ps[:, b, :], ones[:], sq[:, i, :], c == 0, c == DC - 1, reads=[("sq", i), ("const",)], writes=[("ps", b)])
        ACT(rstd[:], ps[:, b, :], AF.Sqrt, reads=[("ps", b), ("eps",)], writes=[("rstd",)], bias=epsb[:, 0:1], scale=1.0 / D)
        P.add("dve", lambda e: e.reciprocal(rstd[:], rstd[:]), [("rstd",)], [("rstd",)])
        for c in range(DC):
            if final:
                STT("dve", xT[:, c, :], xT[:, c, :], gains[:, grow, c:c + 1], rstd[:], ALU.mult, ALU.mult,
                    reads=[("x", c), ("rstd",), ("const",)], writes=[("x", c)])
            else:
                STT("dve", hT[:, c, :], xT[:, c, :], gains[:, grow, c:c + 1], rstd[:], ALU.mult, ALU.mult,
                    reads=[("x", c), ("rstd",), ("const",)], writes=[("h", c)])

    def resid_add(ds, b):
        TT_("dve", xT[:, ds, :], xT[:, ds, :], ps[:, b, :], ALU.add, reads=[("x", ds), ("ps", b)], writes=[("x", ds)])

    def load_consts():
        cl = []

        def CD(out, in_, slow=False):
            r = ("cst", len(cl))
            cl.append(r)
            DMA("sq", out, in_, "const", [], [r], slow=slow)

        CD(ident[:], cid_d)
        CD(identf[:], cidf_d)
        CD(ones[:], cones_d)
        CD(causal[:], ccaus_d)
        CD(qdec[:].rearrange("p h c -> p (h c)"), cqdec_d)
        CD(kdec[:], ckdec_d)
        for i, g in enumerate((nmix_d, nffn_d, nple_d)):
            for l in range(DEPTH):
                CD(gains[:, i * 4 + l, :], g[l:l + 1, :].rearrange("o (c p) -> p (o c)", p=128), slow=True)
        CD(gains[:, 12, :], nfin_d.rearrange("o (c p) -> p (o c)", p=128), slow=True)
        CD(esink[:], ssink_d.partition_broadcast(128))
        P.add("dve", lambda e: e.memset(epsb[:], EPS), cl, [("eps",), ("const",)])
        ACT(esink[:], esink[:], AF.Exp, reads=[("const",)], writes=[("const2",)])

    def load_x(t):
        DMA("act", stage, x_d[t * TT:(t + 1) * TT, :].rearrange("(tb p) d -> p tb d", p=128), "xin",
            reads=[], writes=[("big", c) for c in range(32)])
        for c in range(DC):
            b = next_bank()
            for tb in range(NTB):
                TR(ps[:, b, tb * 128:(tb + 1) * 128], stage[:, tb, c * 128:(c + 1) * 128], identf[:],
                   reads=[("big", tb * 8 + c // 2), ("const",)], writes=[("ps", b)])
            evac_copy(xT[:, c, :], ps[:, b, :], reads=[("ps", b)], writes=[("x", c)])

    def store_out(t):
        for tb in range(NTB):
            for cg in range(4):
                b = next_bank()
                for ci in range(4):
                    c = cg * 4 + ci
                    TR(ps[:, b, ci * 128:(ci + 1) * 128], xT[:, c, tb * 128:(tb + 1) * 128], identf[:],
                       reads=[("x", c), ("const",)], writes=[("ps", b)])
                evac_copy(stage[:, tb, cg * 512:(cg + 1) * 512], ps[:, b, :], reads=[("ps", b)],
                          writes=[("big", tb * 8 + cg * 2), ("big", tb * 8 + cg * 2 + 1)])
            DMA("act", out_d[t * TT + tb * 128:t * TT + (tb + 1) * 128, :], stage[:, tb, :], "oout",
                reads=[("big", tb * 8 + i) for i in range(8)], writes=[("outd",)])

    def ret_proj(t, L, h):
        j = L // 2
        W = rwin_d[j]
        hb = h % 2
        qT, kpp, v, sg = r_qT[hb], r_kpp[hb], r_v[hb], r_sg[hb]
        S = mixf[:, hb, :].rearrange("p (c e) -> p c e", c=2)
        srow = (j * RH + h) * 128
        if t == 0:
            P.add("dve", lambda e, S=S: e.memset(S, 0.0), [], [("S", hb)])
        else:
            DMA("act", S.rearrange("p c e -> p (c e)"), st_d[srow:srow + 128, :], "sld%d" % hb,
                reads=[("std", j, h)], writes=[("S", hb)])
        wv, wr = load_w([W[:, h * RDK:(h + 1) * RDK], W[:, D + h * RDK:D + (h + 1) * RDK]], DC)
        for dc in range(2):
            b = lin_fm(wv, wr, DC, dc, h_act)
            TT_("dve", qT[:, dc, :].rearrange("p (t c) -> p t c", t=NTB), ps[:, b, :].rearrange("p (t c) -> p t c", t=NTB),
                qdec[:, h, :].unsqueeze(1).to_broadcast([128, NTB, 128]), ALU.mult,
                reads=[("ps", b), ("const",)], writes=[("rq", hb)])
            yield
        for tb in range(NTB):
            b = lin_tm(wv, wr, DC, tb, 256, 256, hT, hres)
            ACT(kpp[:, tb, :], ps[:, b, 0:256], AF.Identity, reads=[("ps", b), ("const",)], writes=[("rk", hb)],
                scale=kdec[:, h:h + 1])
            yield
        wv, wr = load_w([W[:, 2 * D + h * RDV:2 * D + (h + 1) * RDV]], DC)
        for tb in range(NTB):
            b = lin_tm(wv, wr, DC, tb, 0, 512, hT, hres)
            evac_copy(v[:, tb, :], ps[:, b, :], reads=[("ps", b)], writes=[("rv", hb)])
            yield
        wv, wr = load_w([W[:, 2 * D + RVW + h * RDV:2 * D + RVW + (h + 1) * RDV]], DC)
        for tb in range(NTB):
            b = lin_tm(wv, wr, DC, tb, 0, 512, hT, hres)
            ACT(sg[:, tb, :], ps[:, b, :], AF.Silu, reads=[("ps", b)], writes=[("rsg", hb)])
            yield
        TT_("dve", sg, sg, r_gn[:, h, :].unsqueeze(1).to_broadcast([128, NTB, RDV]), ALU.mult,
            reads=[("rsg", hb), ("gn",)], writes=[("rsg", hb)])

    def ret_chunks(t, L, h):
        j = L // 2
        hb = h % 2
        qT, kpp, v, sg = r_qT[hb], r_kpp[hb], r_v[hb], r_sg[hb]
        S = mixf[:, hb, :].rearrange("p (c e) -> p c e", c=2)
        srow = (j * RH + h) * 128
        COPY("act", r_Sbf[0], S, reads=[("S", hb)], writes=[("Sbf", 0)])
        for cb in range(NTB):
            par = cb % 2
            kT = r_kT[par]
            Sb_cur = r_Sbf[cb % 2]
            Sb_nxt = r_Sbf[(cb + 1) % 2]
            csl = slice(cb * 128, (cb + 1) * 128)
            psb = ps[:, 6, :].bitcast(BF16)
            for dc in range(2):
                TR(psb[:, dc * 128:(dc + 1) * 128], kpp[:, cb, dc * 128:(dc + 1) * 128], ident[:],
                   reads=[("rk", hb), ("const",)], writes=[("ps", 6)])
            COPY("dve", kT.rearrange("p c t -> p (c t)"), psb[:, 0:256], reads=[("ps", 6)], writes=[("rkT", par)])
            bus = []
            for dc in range(2):
                bu = next_bank()
                bus.append(bu)
                MM(ps[:, bu, :], kpp[:, cb, dc * 128:(dc + 1) * 128], v[:, cb, :], True, True,
                   reads=[("rk", hb), ("rv", hb)], writes=[("ps", bu)])
            yield
            b = next_bank()
            for dc in range(2):
                MM(ps[:, b, 0:128], kT[:, dc, :], qT[:, dc, csl], dc == 0, dc == 1,
                   reads=[("rkT", par), ("rq", hb)], writes=[("ps", b)])
            sT = r_sT[par]
            STT("dve", sT, ps[:, b, 0:128], RET_INV_CD[h], causal[:], ALU.mult, ALU.mult,
                reads=[("ps", b), ("const",)], writes=[("rsT", par)])
            yield
            bo = next_bank()
            for dc in range(2):
                MM(ps[:, bo, :], qT[:, dc, csl], Sb_cur[:, dc, :], dc == 0, False,
                   reads=[("rq", hb), ("Sbf", cb % 2)], writes=[("ps", bo)])
            MM(ps[:, bo, :], sT, v[:, cb, :], False, True, reads=[("rsT", par), ("rv", hb)], writes=[("ps", bo)])
            for dc in range(2):
                STT("dve", S[:, dc, :], S[:, dc, :], RET_CHUNK_DECAY[h], ps[:, bus[dc], :], ALU.mult, ALU.add,
                    reads=[("S", hb), ("ps", bus[dc])], writes=[("S", hb)])
            if cb < NTB - 1:
                COPY("act", Sb_nxt, S, reads=[("S", hb)], writes=[("Sbf", (cb + 1) % 2)])
            yield
            si = ctr["st"]
            ctr["st"] = (si + 1) % 4
            st6 = stat[:, si, 0:6]
            mv = stat[:, si, 6:8]
            rs = stat[:, si, 8:9]
            nb = stat[:, si, 9:10]
            P.add("dve", lambda e, st6=st6, bo=bo: e.bn_stats(st6, ps[:, bo, :]), [("ps", bo)], [("stat", si)])
            P.add("dve", lambda e, st6=st6, mv=mv: e.bn_aggr(mv, st6), [("stat", si)], [("stat", si)])
            ACT(rs, mv[:, 1:2], AF.Sqrt, reads=[("stat", si), ("eps",)], writes=[("stat", si)], bias=epsb[:, 0:1], scale=1.0)
            P.add("dve", lambda e, rs=rs: e.reciprocal(rs, rs), [("stat", si)], [("stat", si)])
            STT("dve", nb, mv[:, 0:1], -1.0, rs, ALU.mult, ALU.mult, reads=[("stat", si)], writes=[("stat", si)])
            yn = r_yn[par]
            ACT(yn, ps[:, bo, :], AF.Identity, reads=[("ps", bo), ("stat", si)], writes=[("ryn", par)], bias=nb, scale=rs)
            z = r_z[par]
            TT_("dve", z, yn, sg[:, cb, :], ALU.mult, reads=[("ryn", par), ("rsg", hb)], writes=[("rz", par)])
            yield
            psb7 = ps[:, 7, :].bitcast(BF16)
            for ec in range(4):
                TR(psb7[:, ec * 128:(ec + 1) * 128], z[:, ec * 128:(ec + 1) * 128], ident[:],
                   reads=[("rz", par), ("const",)], writes=[("ps", 7)])
            COPY("act", big[:, h * 4:(h + 1) * 4, csl], psb7[:, 0:512].rearrange("p (e t) -> p e t", e=4),
                 reads=[("ps", 7)], writes=[("big", h * 4 + ec) for ec in range(4)])
            yield
        DMA("act", st_d[srow:srow + 128, :], S.rearrange("p c e -> p (c e)"), "sst%d" % hb,
            reads=[("S", hb)], writes=[("std", j, h)])

    def ret_layer(t, L):
        j = L // 2
        rmsnorm(0 * 4 + L)
        alias = [("skTp", i) for i in range(5)] + [("svaug", i) for i in range(5)] + [("se", i) for i in range(4)] + [("sotm",), ("skpad",), ("stbl",)]
        DMA("pool", r_gn.rearrange("p h e -> p (h e)"), rgn_d[j:j + 1, :].partition_broadcast(128), "gn",
            reads=[], writes=[("gn",)] + alias)
        for _ in ret_proj(t, L, 0):
            pass
        for h in range(RH):
            A = ret_chunks(t, L, h)
            B = ret_proj(t, L, h + 1) if h + 1 < RH else None
            a_done = False
            b_done = B is None
            while not (a_done and b_done):
                if not a_done:
                    try:
                        next(A)
                    except StopIteration:
                        a_done = True
                if not b_done:
                    try:
                        next(B)
                    except StopIteration:
                        b_done = True
        Wo = rwout_d[j]
        for wt in range(8):
            wv, wr = load_w([Wo[:, wt * 256:(wt + 1) * 256]], 32)
            for dsl in range(2):
                b = lin_fm(wv, wr, 32, dsl, lambda kc: (big[:, kc, :], ("big", kc)))
                resid_add(wt * 2 + dsl, b)

    def swa_layer(t, L):
        j = L // 2
        W = swin_d[j]
        rmsnorm(0 * 4 + L)
        DMA("act", s_tbl.rearrange("p h k q -> p (h k q)"), ctbl_d, "tbl", reads=[], writes=[("stbl",)])
        P.add("dve", lambda e: e.memset(s_kpad.rearrange("p t k a d -> p (t k a d)"), 0.0), [], [("skpad",)])
        P.add("dve", lambda e: e.memset(s_vaug.rearrange("p t k e -> p (t k e)"), 1.0), [], [("svaug", i) for i in range(5)])
        if t > 0:
            COPY("dve", s_kTp[:, 0].rearrange("p k a s -> p (k a s)"), kcar[:, j, :], reads=[("kcar", j)], writes=[("skTp", 0)])
            COPY("dve", s_vaug[:, 0].rearrange("p k e -> p (k e)"), vcar[:, j, :], reads=[("vcar", j)], writes=[("svaug", 0)])
        for wt in range(4):
            wv, wr = load_w([W[:, wt * 512:(wt + 1) * 512]], DC)
            for nsl in range(4):
                b = lin_fm(wv, wr, DC, nsl, h_act)
                evac_copy(big[:, wt * 4 + nsl, :], ps[:, b, :], reads=[("ps", b)], writes=[("big", wt * 4 + nsl)])
        wv, wr = load_w([W[:, D:D + 512]], DC)
        for tb in range(NTB):
            b = lin_tm(wv, wr, DC, tb, 0, 512, hT, hres)
            kin = ps[:, b, 0:256].rearrange("p (k d) -> p k d", k=SKV)
            COPY("act", s_kpad[:, tb, :, 0, 0:64], kin, reads=[("ps", b)], writes=[("skpad",)])
            COPY("dve", s_kpad[:, tb, :, 1, 64:128], kin, reads=[("ps", b)], writes=[("skpad",)])
            COPY("act", s_vaug[:, tb + 1, :, 0:64], ps[:, b, 256:512].rearrange("p (k d) -> p k d", k=SKV),
                 reads=[("ps", b)], writes=[("svaug", tb + 1)])
        for tb in range(NTB):
            bb = 6 + (tb % 2)
            psb = ps[:, bb, :].bitcast(BF16)
            for kh in range(SKV):
                for ab in range(2):
                    i = kh * 2 + ab
                    TR(psb[:, i * 128:(i + 1) * 128], s_kpad[:, tb, kh, ab, :], ident[:],
                       reads=[("skpad",), ("const",)], writes=[("ps", bb)])
            evac_copy(s_kTp[:, tb + 1].rearrange("p k a s -> p (k a s)"), psb[:, 0:1024], reads=[("ps", bb)], writes=[("skTp", tb + 1)])
        for tb in range(NTB):
            qsl = slice(tb * 128, (tb + 1) * 128)
            kts = [1] if (t == 0 and tb == 0) else [0, 1]
            otm = s_otm[0]
            for kh in range(SKV):
                for hh in range(2):
                    h0 = kh * 8 + hh * 4
                    etiles = []
                    for kt in kts:
                        slot = tb + kt
                        b = next_bank()
                        for i in range(4):
                            h = h0 + i
                            MM(ps[:, b, i * 128:(i + 1) * 128], s_kTp[:, slot, kh, h % 2, :], big[:, h // 2, qsl], True, True,
                               reads=[("skTp", slot), ("big", h // 2)], writes=[("ps", b)])
                        ei = ctr["ft"]
                        ctr["ft"] = (ei + 1) % 4
                        e_ = s_e[ei]
                        ACT(e_, ps[:, b, :], AF.Exp, reads=[("ps", b)], writes=[("se", ei)], scale=SHD ** -0.5)
                        TT_("dve", e_.rearrange("p (h q) -> p h q", h=4), e_.rearrange("p (h q) -> p h q", h=4),
                            s_tbl[:, h0:h0 + 4, kt, :], ALU.mult, reads=[("se", ei), ("stbl",)], writes=[("se", ei)])
                        etiles.append((e_, ei, slot))
                    bo = next_bank()
                    for i in range(4):
                        for n, (e_, ei, slot) in enumerate(etiles):
                            MM(ps[:, bo, i * 65:(i + 1) * 65], e_[:, i * 128:(i + 1) * 128], s_vaug[:, slot, kh, :],
                               n == 0, n == len(etiles) - 1, reads=[("se", ei), ("svaug", slot)], writes=[("ps", bo)])
                    si = ctr["sst"]
                    ctr["sst"] = (si + 1) % 4
                    den = sstat[:, si, 0:4]
                    pv = ps[:, bo, 0:260].rearrange("p (h e) -> p h e", e=65)
                    TT_("dve", den, pv[:, :, 64], esink[:, j * SH + h0:j * SH + h0 + 4], ALU.add,
                        reads=[("ps", bo), ("const2",)], writes=[("sstat", si)])
                    P.add("dve", lambda e, den=den: e.reciprocal(den, den), [("sstat", si)], [("sstat", si)])
                    TT_("dve", otm[:, h0 * 64:(h0 + 4) * 64].rearrange("p (h d) -> p h d", h=4), pv[:, :, 0:64],
                        den.unsqueeze(2).to_broadcast([128, 4, 64]), ALU.mult,
                        reads=[("ps", bo), ("sstat", si)], writes=[("sotm",)])
            for g in range(2):
                bb = 6 + g
                psb = ps[:, bb, :].bitcast(BF16)
                for ci in range(8):
                    c = g * 8 + ci
                    TR(psb[:, ci * 128:(ci + 1) * 128], otm[:, c * 128:(c + 1) * 128], ident[:],
                       reads=[("sotm",), ("const",)], writes=[("ps", bb)])
                evac_copy(big[:, 16 + g * 8:16 + (g + 1) * 8, qsl], psb[:, 0:1024].rearrange("p (c q) -> p c q", c=8),
                          reads=[("ps", bb)], writes=[("big", 16 + g * 8 + ci) for ci in range(8)])
        COPY("dve", kcar[:, j, :], s_kTp[:, 4].rearrange("p k a s -> p (k a s)"), reads=[("skTp", 4)], writes=[("kcar", j)])
        COPY("dve", vcar[:, j, :], s_vaug[:, 4].rearrange("p k e -> p (k e)"), reads=[("svaug", 4)], writes=[("vcar", j)])
        Wo = swout_d[j]
        for wt in range(4):
            wv, wr = load_w([Wo[:, wt * 512:(wt + 1) * 512]], DC)
            for nsl in range(4):
                b = lin_fm(wv, wr, DC, nsl, lambda kc: (big[:, 16 + kc, :], ("big", 16 + kc)))
                resid_add(wt * 4 + nsl, b)

    def ffn_layer(t, L):
        W = fwin_d[L]
        Wo = fwout_d[L]
        rmsnorm(1 * 4 + L)
        HALF = FH // 2
        for half in range(2):
            for jt in range(11):
                c0 = half * HALF + jt * 256
                wv, wr = load_w([W[:, c0:c0 + 256], W[:, FH + c0:FH + c0 + 256]], DC)
                for jj in range(2):
                    hc = jt * 2 + jj
                    bg = lin_fm(wv, wr, DC, jj, h_act)
                    bu = lin_fm(wv, wr, DC, jj, h_act, col0=256)
                    fi = ctr["sst"] % 2
                    ctr["sst"] = (ctr["sst"] + 1) % 4
                    ACT(ftmp[:, fi, :], ps[:, bg, :], AF.Silu, reads=[("ps", bg)], writes=[("ft", fi)])
                    TT_("dve", big[:, hc, :], ftmp[:, fi, :], ps[:, bu, :], ALU.mult,
                        reads=[("ft", fi), ("ps", bu)], writes=[("big", hc)])
            for wt in range(8):
                wv, wr = load_w([Wo[half * HALF:(half + 1) * HALF, wt * 256:(wt + 1) * 256]], 22)
                for dsl in range(2):
                    b = lin_fm(wv, wr, 22, dsl, lambda kc: (big[:, kc, :], ("big", kc)))
                    resid_add(wt * 2 + dsl, b)

    def ple_layer(t, L):
        rmsnorm(2 * 4 + L)
        pst = bigf[:, 0:NTB * PLE].rearrange("p (t d) -> p t d", t=NTB)
        pbf = big[:].rearrange("p a b -> p (a b)")[:, 2048:2048 + NTB * PLE].rearrange("p (t d) -> p t d", t=NTB)
        pT = big[:, 6:8, :]
        DMA("act", pst, p_d[L, t * TT:(t + 1) * TT, :].rearrange("(tb p) d -> p tb d", p=128), "pin",
            reads=[], writes=[("big", c) for c in range(4)])
        COPY("dve", pbf, pst, reads=[("big", c) for c in range(4)], writes=[("big", 4), ("big", 5)])
        psb = ps[:, 6, :].bitcast(BF16)
        for tb in range(NTB):
            for kc in range(2):
                TR(psb[:, kc * 512 + tb * 128:kc * 512 + (tb + 1) * 128], pbf[:, tb, kc * 128:(kc + 1) * 128], ident[:],
                   reads=[("big", 4), ("big", 5), ("const",)], writes=[("ps", 6)])
        COPY("act", pT, psb[:, 0:1024].rearrange("p (c t) -> p c t", c=2), reads=[("ps", 6)], writes=[("big", 6), ("big", 7)])
        Wg = pgate_d[L]
        Wp = pproj_d[L]
        for wt in range(4):
            wv, wr = load_w([Wg[:, wt * 512:(wt + 1) * 512]], DC)
            wpv, wpr = load_w([Wp[:, wt * 512:(wt + 1) * 512]], 2)
            for nsl in range(4):
                ds = wt * 4 + nsl
                bg = lin_fm(wv, wr, DC, nsl, h_act)
                bp = lin_fm(wpv, wpr, 2, nsl, lambda kc: (pT[:, kc, :], ("big", 6 + kc)))
                fi = ctr["sst"] % 2
                ctr["sst"] = (ctr["sst"] + 1) % 4
                ACT(ftmp[:, fi, :], ps[:, bg, :], AF.Sigmoid, reads=[("ps", bg)], writes=[("ft", fi)])
                TT_("dve", ftmp[:, fi, :], ftmp[:, fi, :], ps[:, bp, :], ALU.mult, reads=[("ft", fi), ("ps", bp)], writes=[("ft", fi)])
                TT_("dve", xT[:, ds, :], xT[:, ds, :], ftmp[:, fi, :], ALU.add, reads=[("x", ds), ("ft", fi)], writes=[("x", ds)])

    load_consts()
    for t in range(NT):
        cur["t"] = t
        cur["u"] = 0
        load_x(t)
        for L in range(NL):
            if L % 2 == 0:
                ret_layer(t, L)
            else:
                swa_layer(t, L)
            ffn_layer(t, L)
            ple_layer(t, L)
        rmsnorm(12, final=True)
        store_out(t)
    counts = P.finalize()
    sems = {}
    for name in counts:
        sems[name] = es.enter_context(nc.semaphore("s_" + name))
    with nc.Block() as block:
        @block.sync
        def _(e):
            P.emit("sq", e, sems)
            e.wait_ge(sems["oout"], counts["oout"])

        @block.gpsimd
        def _(e):
            P.emit("pool", e, sems)

        @block.tensor
        def _(e):
            P.emit("pe", e, sems)

        @block.scalar
        def _(e):
            P.emit("act", e, sems)

        @block.vector
        def _(e):
            P.emit("dve", e, sems)
    es.close()
    return nc, len(P.ops)


def make_in_maps(inputs, cores):
    cst = _consts()
    shared = {
        "norm_mix": np.ascontiguousarray(inputs["norm_mix"], dtype=np.float32),
        "norm_ffn": np.ascontiguousarray(inputs["norm_ffn"], dtype=np.float32),
        "norm_ple": np.ascontiguousarray(inputs["norm_ple"], dtype=np.float32),
        "norm_final": np.ascontiguousarray(inputs["norm_final"], dtype=np.float32).reshape(1, D),
        "ret_w_in": np.ascontiguousarray(inputs["ret_w_in"], dtype=np.float32),
        "ret_gn": np.ascontiguousarray(inputs["ret_gn"], dtype=np.float32).reshape(2, RH * RDV),
        "ret_w_out": np.ascontiguousarray(inputs["ret_w_out"], dtype=np.float32),
        "swa_w_in": np.ascontiguousarray(inputs["swa_w_in"], dtype=np.float32),
        "swa_sinks": np.ascontiguousarray(inputs["swa_sinks"], dtype=np.float32).reshape(1, 2 * SH),
        "swa_w_out": np.ascontiguousarray(inputs["swa_w_out"], dtype=np.float32),
        "ffn_w_in": np.ascontiguousarray(inputs["ffn_w_in"], dtype=np.float32),
        "ffn_w_out": np.ascontiguousarray(inputs["ffn_w_out"], dtype=np.float32),
        "ple_w_proj": np.ascontiguousarray(inputs["ple_w_proj"], dtype=np.float32),
        "ple_w_gate": np.ascontiguousarray(inputs["ple_w_gate"], dtype=np.float32),
    }
    shared.update(cst)
    x = np.asarray(inputs["x"])
    p = np.asarray(inputs["p"])
    maps = []
    for b in cores:
        m = dict(shared)
        m["x"] = np.ascontiguousarray(x[b], dtype=np.float32)
        m["p"] = np.ascontiguousarray(p[:, b], dtype=np.float32)
        maps.append(m)
    return maps


def kernel(**inputs):
    nc, _ = build()
    B = np.asarray(inputs["x"]).shape[0]
    in_maps = make_in_maps(inputs, list(range(B)))
    res = run_bass_kernel_spmd(nc, in_maps, core_ids=list(range(B)))
    out = np.stack([np.asarray(r["out"], dtype=np.float32) for r in res.results], axis=0)
    return out
```

```python
import math
from contextlib import ExitStack
from functools import partial

import numpy as np
import ml_dtypes

import concourse.bass as bass
import concourse.mybir as mybir
from concourse.bass_utils import run_bass_kernel_spmd

F32 = mybir.dt.float32
BF16 = mybir.dt.bfloat16
ALU = mybir.AluOpType
AF = mybir.ActivationFunctionType

D = 2048
SEQ = 4096
DEPTH = 4
TT = 512
NTB = TT // 128
DC = D // 128
RH, RDK, RDV = 8, 256, 512
RET_IN = 12288
RVW = 4096
SH, SKV, SHD = 32, 4, 64
SWA_IN = 2560
FH = 5632
PLE = 256
EPS = 1e-6
NWS = 3
WSLOT = 8192


class Op:
    __slots__ = ("eng", "fn", "deps", "sem", "inc", "need", "cnt", "isdma")


class Prog:
    def __init__(self):
        self.ops = []
        self.last_w = {}
        self.readers = {}

    def add(self, eng, fn, reads=(), writes=(), dma_sem=None):
        op = Op()
        op.eng = eng
        op.fn = fn
        op.isdma = dma_sem is not None
        op.sem = dma_sem if op.isdma else eng
        op.inc = 16 if op.isdma else 1
        op.need = op.isdma
        op.cnt = 0
        deps = []
        lw, rd = self.last_w, self.readers
        for r in reads:
            w = lw.get(r)
            if w is not None:
                deps.append(w)
        for r in writes:
            w = lw.get(r)
            if w is not None:
                deps.append(w)
            rs = rd.get(r)
            if rs:
                deps.extend(rs)
        for r in reads:
            rd.setdefault(r, []).append(op)
        for r in writes:
            lw[r] = op
            rd[r] = []
        dd = []
        seen = set()
        for p in deps:
            if id(p) in seen:
                continue
            seen.add(id(p))
            if (not p.isdma) and (not op.isdma) and p.eng == "pe" and eng == "pe":
                continue
            p.need = True
            dd.append(p)
        op.deps = dd
        self.ops.append(op)
        return op

    def finalize(self):
        counts = {}
        for op in self.ops:
            if op.need:
                counts[op.sem] = counts.get(op.sem, 0) + op.inc
                op.cnt = counts[op.sem]
        return counts

    def emit(self, eng_name, eng, sems):
        waited = {}
        for op in self.ops:
            if op.eng != eng_name:
                continue
            need = {}
            for p in op.deps:
                if p.cnt > need.get(p.sem, 0):
                    need[p.sem] = p.cnt
            for key, cnt in need.items():
                if waited.get(key, 0) >= cnt:
                    continue
                waited[key] = cnt
                eng.wait_ge(sems[key], cnt)
            ins = op.fn(eng)
            if op.need:
                ins.then_inc(sems[op.sem], op.inc)


def _consts():
    bf = ml_dtypes.bfloat16
    c = {}
    c["c_ident"] = np.eye(128, dtype=np.float32).astype(bf)
    c["c_identf"] = np.eye(128, dtype=np.float32)
    c["c_ones"] = np.ones((128, 128), dtype=np.float32).astype(bf)
    m = np.arange(128)[:, None]
    q = np.arange(128)[None, :]
    c["c_causal"] = (q >= m).astype(np.float32).astype(bf)
    log_g = np.log1p(-np.exp2(-5.0 - np.arange(RH, dtype=np.float64)))
    pos = np.arange(128, dtype=np.float64)
    qdec = np.exp(log_g[:, None] * (pos[None, :] + 1.0))
    c["c_qdec"] = np.broadcast_to(qdec.reshape(1, RH * 128), (128, RH * 128)).astype(np.float32).copy()
    kdec = (RDK ** -0.5) * np.exp(log_g[None, :] * (127.0 - pos[:, None]))
    c["c_kdec"] = kdec.astype(np.float32).copy()
    slopes = np.exp2(-8.0 * (np.arange(SH, dtype=np.float64) + 1.0) / SH)
    s = np.arange(128)[:, None]
    qq = np.arange(128)[None, :]
    tbl = np.zeros((128, SH, 2, 128), dtype=np.float64)
    rel_prev = (qq - s + 128).astype(np.float64)
    rel_cur = (qq - s).astype(np.float64)
    for h in range(SH):
        tbl[:, h, 0, :] = np.where(qq < s, np.exp(-slopes[h] * rel_prev), 0.0)
        tbl[:, h, 1, :] = np.where(qq >= s, np.exp(-slopes[h] * rel_cur), 0.0)
    c["c_swatbl"] = tbl.reshape(128, SH * 2 * 128).astype(np.float32).astype(bf)
    return c


RET_GAMMA = [1.0 - 2.0 ** (-5.0 - h) for h in range(RH)]
RET_CHUNK_DECAY = [float(np.float32(np.exp(np.float64(128.0) * np.log1p(-np.exp2(-5.0 - h))))) for h in range(RH)]
RET_INV_CD = [float(np.exp(-np.float64(128.0) * np.log1p(-np.exp2(-5.0 - h)))) for h in range(RH)]


def build(NT=SEQ // TT, NL=DEPTH):
    nc = bass.Bass("TRN2", target_bir_lowering=False)
    P = Prog()
    es = ExitStack()

    def din(name, shape, dt=F32):
        return nc.dram_tensor(name, list(shape), dt, kind="ExternalInput").ap()

    x_d = din("x", [SEQ, D])
    p_d = din("p", [DEPTH, SEQ, PLE])
    nmix_d = din("norm_mix", [DEPTH, D])
    nffn_d = din("norm_ffn", [DEPTH, D])
    nple_d = din("norm_ple", [DEPTH, D])
    nfin_d = din("norm_final", [1, D])
    rwin_d = din("ret_w_in", [2, D, RET_IN])
    rgn_d = din("ret_gn", [2, RH * RDV])
    rwout_d = din("ret_w_out", [2, RVW, D])
    swin_d = din("swa_w_in", [2, D, SWA_IN])
    ssink_d = din("swa_sinks", [1, 2 * SH])
    swout_d = din("swa_w_out", [2, D, D])
    fwin_d = din("ffn_w_in", [DEPTH, D, 2 * FH])
    fwout_d = din("ffn_w_out", [DEPTH, FH, D])
    pproj_d = din("ple_w_proj", [DEPTH, PLE, D])
    pgate_d = din("ple_w_gate", [DEPTH, D, D])
    cid_d = din("c_ident", [128, 128], BF16)
    cidf_d = din("c_identf", [128, 128])
    cones_d = din("c_ones", [128, 128], BF16)
    ccaus_d = din("c_causal", [128, 128], BF16)
    cqdec_d = din("c_qdec", [128, RH * 128])
    ckdec_d = din("c_kdec", [128, RH])
    ctbl_d = din("c_swatbl", [128, SH * 2 * 128], BF16)
    out_d = nc.dram_tensor("out", [SEQ, D], F32, kind="ExternalOutput").ap()
    st_d = nc.dram_tensor("state_scratch", [2 * RH * 128, 2 * RDV], F32, kind="Internal").ap()
    WCAP = 1000000
    wbf_ds = [nc.dram_tensor("wbf_scratch%d" % i, [128, WCAP], BF16, kind="Internal").ap() for i in range(2)]
    uoff = []
    walloc = {"i": 0, "off": 0}

    def sb(name, shape, dt):
        return es.enter_context(nc.sbuf_tensor(name, list(shape), dt))

    xT = sb("xT", [128, DC, TT], F32)
    hT = sb("hT", [128, DC, TT], BF16)
    big = sb("big", [128, 32, TT], BF16)
    wsl = sb("wsl", [128, NWS, WSLOT], BF16)
    mixb = sb("mixb", [128, 23040], BF16)
    mixf = sb("mixf", [128, 2, 2 * RDV], F32)
    gains = sb("gains", [128, 13, DC], F32)
    ident = sb("ident", [128, 128], BF16)
    identf = sb("identf", [128, 128], F32)
    ones = sb("ones", [128, 128], BF16)
    causal = sb("causal", [128, 128], BF16)
    qdec = sb("qdec", [128, RH, 128], F32)
    kdec = sb("kdec", [128, RH], F32)
    esink = sb("esink", [128, 2 * SH], F32)
    sq = sb("sq", [128, 2, TT], BF16)
    rstd = sb("rstd", [128, TT], F32)
    ftmp = sb("ftmp", [128, 2, TT], F32)
    kcar = sb("kcar", [128, 2, SKV * 2 * 128], BF16)
    vcar = sb("vcar", [128, 2, SKV * 65], BF16)
    stat = sb("stat", [128, 4, 16], F32)
    sstat = sb("sstat", [128, 4, 8], F32)
    epsb = sb("epsb", [128, 8], F32)
    ps = es.enter_context(nc.psum_tensor("ps", [128, 8, 512], F32))

    bigf = big[:].rearrange("p a b -> p (a b)").bitcast(F32)
    stage = bigf.rearrange("p (t d) -> p t d", t=NTB)

    off = [0]

    def carve(n, shape_str=None, **kw):
        a = mixb[:, off[0]:off[0] + n]
        off[0] += n
        if shape_str:
            a = a.rearrange(shape_str, **kw)
        return a

    r_qT = [carve(2 * TT, "p (c t) -> p c t", c=2) for _ in range(2)]
    r_kpp = [carve(NTB * RDK, "p (t d) -> p t d", t=NTB) for _ in range(2)]
    r_v = [carve(NTB * RDV, "p (t e) -> p t e", t=NTB) for _ in range(2)]
    r_sg = [carve(NTB * RDV, "p (t e) -> p t e", t=NTB) for _ in range(2)]
    r_kT = [carve(2 * 128, "p (c t) -> p c t", c=2) for _ in range(2)]
    r_sT = [carve(128) for _ in range(2)]
    r_Sbf = [carve(2 * RDV, "p (c e) -> p c e", c=2) for _ in range(2)]
    r_yn = [carve(RDV) for _ in range(2)]
    r_z = [carve(RDV) for _ in range(2)]
    r_gn = carve(RH * RDV, "p (h e) -> p h e", h=RH)
    ret_end = off[0]
    off[0] = 0
    s_tbl = carve(SH * 2 * 128, "p (h k q) -> p h k q", h=SH, k=2)
    s_kpad = carve(NTB * SKV * 2 * 128, "p (t k a d) -> p t k a d", t=NTB, k=SKV, a=2)
    s_kTp = carve(5 * SKV * 2 * 128, "p (t k a s) -> p t k a s", t=5, k=SKV, a=2)
    s_vaug = carve(5 * SKV * 65, "p (t k e) -> p t k e", t=5, k=SKV)
    s_e = [carve(512) for _ in range(4)]
    s_otm = [carve(D) for _ in range(1)]
    swa_end = off[0]
    assert max(ret_end, swa_end) <= 23040, (ret_end, swa_end)

    ctr = {"bank": 0, "ws": 0, "sq": 0, "ft": 0, "alt": 0, "st": 0, "sst": 0}

    def next_bank():
        b = ctr["bank"]
        ctr["bank"] = (b + 1) % 6
        return b

    def alt_engine():
        ctr["alt"] ^= 1
        return "act" if ctr["alt"] else "dve"

    def MM(out, lhsT, rhs, start, stop, reads, writes):
        P.add("pe", lambda e: e.matmul(out, lhsT, rhs, start=start, stop=stop), reads, writes)

    def TR(out, in_, idt, reads, writes):
        P.add("pe", lambda e: e.transpose(out, in_, idt), reads, writes)

    def ACT(out, in_, func, reads, writes, bias=None, scale=None):
        kw = {}
        if bias is not None:
            kw["bias"] = bias
        if scale is not None:
            kw["scale"] = scale
        P.add("act", lambda e: e.activation(out, in_, func, **kw), reads, writes)

    def TT_(eng, out, in0, in1, op, reads, writes):
        P.add(eng, lambda e: e.tensor_tensor(out, in0, in1, op), reads, writes)

    def TS(eng, out, in0, s1, s2, op0, op1, reads, writes):
        if op1 is None:
            P.add(eng, lambda e: e.tensor_scalar(out, in0, s1, None, op0), reads, writes)
        else:
            P.add(eng, lambda e: e.tensor_scalar(out, in0, s1, s2, op0, op1), reads, writes)

    def STT(eng, out, in0, scalar, in1, op0, op1, reads, writes):
        P.add(eng, lambda e: e.scalar_tensor_tensor(out, in0, scalar, in1, op0, op1), reads, writes)

    def COPY(eng, out, in_, reads, writes):
        if eng == "act":
            P.add("act", lambda e: e.copy(out, in_), reads, writes)
        else:
            P.add(eng, lambda e: e.tensor_copy(out, in_), reads, writes)

    def DMA(q, out, in_, sem, reads, writes, slow=False):
        if slow:
            P.add(q, lambda e: e.dma_start(out=out, in_=in_, allow_slow_non_contiguous=True), reads, writes, dma_sem=sem)
        else:
            P.add(q, lambda e: e.dma_start(out=out, in_=in_), reads, writes, dma_sem=sem)

    def evac_copy(out, in_, reads, writes):
        COPY(alt_engine(), out, in_, reads, writes)

    cur = {"t": 0, "u": 0}

    def load_w(parts, KC):
        s = ctr["ws"]
        ctr["ws"] = (s + 1) % NWS
        u = cur["u"]
        cur["u"] += 1
        ntot = sum(pp.shape[1] for pp in parts)
        nel = KC * ntot
        assert nel <= WSLOT, (KC, ntot)
        view = wsl[:, s, 0:nel].rearrange("p (k n) -> p k n", k=KC)
        if cur["t"] == 0:
            if walloc["off"] + nel > WCAP:
                walloc["i"] += 1
                walloc["off"] = 0
            assert walloc["i"] < len(wbf_ds)
            uoff.append((walloc["i"], walloc["off"]))
            walloc["off"] += nel
        wi, wo = uoff[u]
        scr = wbf_ds[wi][:, wo:wo + nel]
        if cur["t"] == 0:
            o = 0
            for pp in parts:
                n = pp.shape[1]
                DMA("pool", view[:, :, o:o + n], pp.rearrange("(k p) n -> p k n", p=128), "w%d" % s,
                    reads=[], writes=[("w", s)])
                o += n
            if NT > 1:
                DMA("sq", scr, wsl[:, s, 0:KC * ntot], "wb%d" % s, reads=[("w", s)], writes=[("wbf", u)])
        else:
            DMA("sq", wsl[:, s, 0:KC * ntot], scr, "v%d" % s, reads=[("wbf", u)], writes=[("w", s)])
        return view, ("w", s)

    def lin_fm(wview, wres, KC, nsl, act_fn, col0=0):
        b = next_bank()
        for kc in range(KC):
            a, ares = act_fn(kc)
            MM(ps[:, b, :], wview[:, kc, col0 + nsl * 128:col0 + (nsl + 1) * 128], a, kc == 0, kc == KC - 1,
               reads=[wres, ares], writes=[("ps", b)])
        return b

    def lin_fm_g(out, wview, wres, KC, nsl, act_fn, col0=0):
        b = next_bank()
        out.append(b)
        for kc in range(KC):
            a, ares = act_fn(kc)
            MM(ps[:, b, :], wview[:, kc, col0 + nsl * 128:col0 + (nsl + 1) * 128], a, kc == 0, kc == KC - 1,
               reads=[wres, ares], writes=[("ps", b)])
            if kc == KC // 2 - 1:
                yield

    def lin_tm_g(out, wview, wres, KC, tb, c0, n, act3, actres_fn):
        b = next_bank()
        out.append(b)
        for kc in range(KC):
            MM(ps[:, b, 0:n], act3[:, kc, tb * 128:(tb + 1) * 128], wview[:, kc, c0:c0 + n], kc == 0, kc == KC - 1,
               reads=[wres, actres_fn(kc)], writes=[("ps", b)])
            if kc == KC // 2 - 1:
                yield

    def lin_tm(wview, wres, KC, tb, c0, n, act3, actres_fn):
        b = next_bank()
        for kc in range(KC):
            MM(ps[:, b, 0:n], act3[:, kc, tb * 128:(tb + 1) * 128], wview[:, kc, c0:c0 + n], kc == 0, kc == KC - 1,
               reads=[wres, actres_fn(kc)], writes=[("ps", b)])
        return b

    def hres(kc):
        return ("h", kc)

    def h_act(kc):
        return hT[:, kc, :], ("h", kc)

    def rmsnorm(grow, final=False):
        b = next_bank()
        for c in range(DC):
            i = ctr["sq"]
            ctr["sq"] ^= 1
            ACT(sq[:, i, :], xT[:, c, :], AF.Square, reads=[("x", c)], writes=[("sq", i)])
            MM(ps[:, b, :], ones[:], sq[:, i, :], c == 0, c == DC - 1, reads=[("sq", i), ("const",)], writes=[("ps", b)])
        ACT(rstd[:], ps[:, b, :], AF.Sqrt, reads=[("ps", b), ("eps",)], writes=[("rstd",)], bias=epsb[:, 0:1], scale=1.0 / D)
        P.add("dve", lambda e: e.reciprocal(rstd[:], rstd[:]), [("rstd",)], [("rstd",)])
        for c in range(DC):
            if final:
                STT("dve", xT[:, c, :], xT[:, c, :], gains[:, grow, c:c + 1], rstd[:], ALU.mult, ALU.mult,
                    reads=[("x", c), ("rstd",), ("const",)], writes=[("x", c)])
            else:
                STT("dve", hT[:, c, :], xT[:, c, :], gains[:, grow, c:c + 1], rstd[:], ALU.mult, ALU.mult,
                    reads=[("x", c), ("rstd",), ("const",)], writes=[("h", c)])

    def resid_add(ds, b):
        TT_("dve", xT[:, ds, :], xT[:, ds, :], ps[:, b, :], ALU.add, reads=[("x", ds), ("ps", b)], writes=[("x", ds)])

    def load_consts():
        cl = []

        def CD(out, in_, slow=False):
            r = ("cst", len(cl))
            cl.append(r)
            DMA("sq", out, in_, "const", [], [r], slow=slow)

        CD(ident[:], cid_d)
        CD(identf[:], cidf_d)
        CD(ones[:], cones_d)
        CD(causal[:], ccaus_d)
        CD(qdec[:].rearrange("p h c -> p (h c)"), cqdec_d)
        CD(kdec[:], ckdec_d)
        for i, g in enumerate((nmix_d, nffn_d, nple_d)):
            for l in range(DEPTH):
                CD(gains[:, i * 4 + l, :], g[l:l + 1, :].rearrange("o (c p) -> p (o c)", p=128), slow=True)
        CD(gains[:, 12, :], nfin_d.rearrange("o (c p) -> p (o c)", p=128), slow=True)
        CD(esink[:], ssink_d.partition_broadcast(128))
        P.add("dve", lambda e: e.memset(epsb[:], EPS), cl, [("eps",), ("const",)])
        ACT(esink[:], esink[:], AF.Exp, reads=[("const",)], writes=[("const2",)])

    def load_x(t):
        DMA("act", stage, x_d[t * TT:(t + 1) * TT, :].rearrange("(tb p) d -> p tb d", p=128), "xin",
            reads=[], writes=[("big", c) for c in range(32)])
        for c in range(DC):
            b = next_bank()
            for tb in range(NTB):
                TR(ps[:, b, tb * 128:(tb + 1) * 128], stage[:, tb, c * 128:(c + 1) * 128], identf[:],
                   reads=[("big", tb * 8 + c // 2), ("const",)], writes=[("ps", b)])
            evac_copy(xT[:, c, :], ps[:, b, :], reads=[("ps", b)], writes=[("x", c)])

    def store_out(t):
        for tb in range(NTB):
            for cg in range(4):
                b = next_bank()
                for ci in range(4):
                    c = cg * 4 + ci
                    TR(ps[:, b, ci * 128:(ci + 1) * 128], xT[:, c, tb * 128:(tb + 1) * 128], identf[:],
                       reads=[("x", c), ("const",)], writes=[("ps", b)])
                evac_copy(stage[:, tb, cg * 512:(cg + 1) * 512], ps[:, b, :], reads=[("ps", b)],
                          writes=[("big", tb * 8 + cg * 2), ("big", tb * 8 + cg * 2 + 1)])
            DMA("act", out_d[t * TT + tb * 128:t * TT + (tb + 1) * 128, :], stage[:, tb, :], "oout",
                reads=[("big", tb * 8 + i) for i in range(8)], writes=[("outd",)])

    def ret_proj(t, L, h):
        j = L // 2
        W = rwin_d[j]
        hb = h % 2
        qT, kpp, v, sg = r_qT[hb], r_kpp[hb], r_v[hb], r_sg[hb]
        S = mixf[:, hb, :].rearrange("p (c e) -> p c e", c=2)
        srow = (j * RH + h) * 128
        if t == 0:
            P.add("dve", lambda e, S=S: e.memset(S, 0.0), [], [("S", hb)])
        else:
            DMA("act", S.rearrange("p c e -> p (c e)"), st_d[srow:srow + 128, :], "sld%d" % hb,
                reads=[("std", j, h)], writes=[("S", hb)])
        wv, wr = load_w([W[:, h * RDK:(h + 1) * RDK], W[:, D + h * RDK:D + (h + 1) * RDK]], DC)
        for dc in range(2):
            bl = []
            yield from lin_fm_g(bl, wv, wr, DC, dc, h_act)
            b = bl[0]
            TT_("dve", qT[:, dc, :].rearrange("p (t c) -> p t c", t=NTB), ps[:, b, :].rearrange("p (t c) -> p t c", t=NTB),
                qdec[:, h, :].unsqueeze(1).to_broadcast([128, NTB, 128]), ALU.mult,
                reads=[("ps", b), ("const",)], writes=[("rq", hb)])
            yield
        for tb in range(NTB):
            bl = []
            yield from lin_tm_g(bl, wv, wr, DC, tb, 256, 256, hT, hres)
            b = bl[0]
            ACT(kpp[:, tb, :], ps[:, b, 0:256], AF.Identity, reads=[("ps", b), ("const",)], writes=[("rk", hb)],
                scale=kdec[:, h:h + 1])
            yield
        wv, wr = load_w([W[:, 2 * D + h * RDV:2 * D + (h + 1) * RDV]], DC)
        for tb in range(NTB):
            bl = []
            yield from lin_tm_g(bl, wv, wr, DC, tb, 0, 512, hT, hres)
            b = bl[0]
            evac_copy(v[:, tb, :], ps[:, b, :], reads=[("ps", b)], writes=[("rv", hb)])
            yield
        wv, wr = load_w([W[:, 2 * D + RVW + h * RDV:2 * D + RVW + (h + 1) * RDV]], DC)
        for tb in range(NTB):
            bl = []
            yield from lin_tm_g(bl, wv, wr, DC, tb, 0, 512, hT, hres)
            b = bl[0]
            ACT(sg[:, tb, :], ps[:, b, :], AF.Silu, reads=[("ps", b)], writes=[("rsg", hb)])
            yield
        TT_("dve", sg, sg, r_gn[:, h, :].unsqueeze(1).to_broadcast([128, NTB, RDV]), ALU.mult,
            reads=[("rsg", hb), ("gn",)], writes=[("rsg", hb)])

    def ret_chunks(t, L, h):
        j = L // 2
        hb = h % 2
        qT, kpp, v, sg = r_qT[hb], r_kpp[hb], r_v[hb], r_sg[hb]
        S = mixf[:, hb, :].rearrange("p (c e) -> p c e", c=2)
        srow = (j * RH + h) * 128
        COPY("act", r_Sbf[0], S, reads=[("S", hb)], writes=[("Sbf", 0)])
        for cb in range(NTB):
            par = cb % 2
            kT = r_kT[par]
            Sb_cur = r_Sbf[cb % 2]
            Sb_nxt = r_Sbf[(cb + 1) % 2]
            csl = slice(cb * 128, (cb + 1) * 128)
            psb = ps[:, 6, :].bitcast(BF16)
            for dc in range(2):
                TR(psb[:, dc * 128:(dc + 1) * 128], kpp[:, cb, dc * 128:(dc + 1) * 128], ident[:],
                   reads=[("rk", hb), ("const",)], writes=[("ps", 6)])
            COPY("dve", kT.rearrange("p c t -> p (c t)"), psb[:, 0:256], reads=[("ps", 6)], writes=[("rkT", par)])
            bus = []
            for dc in range(2):
                bu = next_bank()
                bus.append(bu)
                MM(ps[:, bu, :], kpp[:, cb, dc * 128:(dc + 1) * 128], v[:, cb, :], True, True,
                   reads=[("rk", hb), ("rv", hb)], writes=[("ps", bu)])
            yield
            b = next_bank()
            for dc in range(2):
                MM(ps[:, b, 0:128], kT[:, dc, :], qT[:, dc, csl], dc == 0, dc == 1,
                   reads=[("rkT", par), ("rq", hb)], writes=[("ps", b)])
            sT = r_sT[par]
            STT("dve", sT, ps[:, b, 0:128], RET_INV_CD[h], causal[:], ALU.mult, ALU.mult,
                reads=[("ps", b), ("const",)], writes=[("rsT", par)])
            yield
            bo = next_bank()
            for dc in range(2):
                MM(ps[:, bo, :], qT[:, dc, csl], Sb_cur[:, dc, :], dc == 0, False,
                   reads=[("rq", hb), ("Sbf", cb % 2)], writes=[("ps", bo)])
            MM(ps[:, bo, :], sT, v[:, cb, :], False, True, reads=[("rsT", par), ("rv", hb)], writes=[("ps", bo)])
            for dc in range(2):
                STT("dve", S[:, dc, :], S[:, dc, :], RET_CHUNK_DECAY[h], ps[:, bus[dc], :], ALU.mult, ALU.add,
                    reads=[("S", hb), ("ps", bus[dc])], writes=[("S", hb)])
            if cb < NTB - 1:
                COPY("act", Sb_nxt, S, reads=[("S", hb)], writes=[("Sbf", (cb + 1) % 2)])
            yield
            si = ctr["st"]
            ctr["st"] = (si + 1) % 4
            st6 = stat[:, si, 0:6]
            mv = stat[:, si, 6:8]
            rs = stat[:, si, 8:9]
            nb = stat[:, si, 9:10]
            P.add("dve", lambda e, st6=st6, bo=bo: e.bn_stats(st6, ps[:, bo, :]), [("ps", bo)], [("stat", si)])
            P.add("dve", lambda e, st6=st6, mv=mv: e.bn_aggr(mv, st6), [("stat", si)], [("stat", si)])
            ACT(rs, mv[:, 1:2], AF.Sqrt, reads=[("stat", si), ("eps",)], writes=[("stat", si)], bias=epsb[:, 0:1], scale=1.0)
            P.add("dve", lambda e, rs=rs: e.reciprocal(rs, rs), [("stat", si)], [("stat", si)])
            STT("dve", nb, mv[:, 0:1], -1.0, rs, ALU.mult, ALU.mult, reads=[("stat", si)], writes=[("stat", si)])
            yn = r_yn[par]
            ACT(yn, ps[:, bo, :], AF.Identity, reads=[("ps", bo), ("stat", si)], writes=[("ryn", par)], bias=nb, scale=rs)
            z = r_z[par]
            TT_("dve", z, yn, sg[:, cb, :], ALU.mult, reads=[("ryn", par), ("rsg", hb)], writes=[("rz", par)])
            yield
            psb7 = ps[:, 7, :].bitcast(BF16)
            for ec in range(4):
                TR(psb7[:, ec * 128:(ec + 1) * 128], z[:, ec * 128:(ec + 1) * 128], ident[:],
                   reads=[("rz", par), ("const",)], writes=[("ps", 7)])
            COPY("act", big[:, h * 4:(h + 1) * 4, csl], psb7[:, 0:512].rearrange("p (e t) -> p e t", e=4),
                 reads=[("ps", 7)], writes=[("big", h * 4 + ec) for ec in range(4)])
            yield
        DMA("act", st_d[srow:srow + 128, :], S.rearrange("p c e -> p (c e)"), "sst%d" % hb,
            reads=[("S", hb)], writes=[("std", j, h)])

    def ret_layer(t, L):
        j = L // 2
        rmsnorm(0 * 4 + L)
        alias = [("skTp", i) for i in range(5)] + [("svaug", i) for i in range(5)] + [("se", i) for i in range(4)] + [("sotm",), ("skpad",), ("stbl",)]
        DMA("pool", r_gn.rearrange("p h e -> p (h e)"), rgn_d[j:j + 1, :].partition_broadcast(128), "gn",
            reads=[], writes=[("gn",)] + alias)
        for _ in ret_proj(t, L, 0):
            pass
        for h in range(RH):
            A = ret_chunks(t, L, h)
            B = ret_proj(t, L, h + 1) if h + 1 < RH else None
            a_done = False
            b_done = B is None
            while not (a_done and b_done):
                if not a_done:
                    try:
                        next(A)
                    except StopIteration:
                        a_done = True
                if not b_done:
                    try:
                        next(B)
                    except StopIteration:
                        b_done = True
        Wo = rwout_d[j]
        for wt in range(8):
            wv, wr = load_w([Wo[:, wt * 256:(wt + 1) * 256]], 32)
            for dsl in range(2):
                b = lin_fm(wv, wr, 32, dsl, lambda kc: (big[:, kc, :], ("big", kc)))
                resid_add(wt * 2 + dsl, b)

    def swa_layer(t, L):
        j = L // 2
        W = swin_d[j]
        rmsnorm(0 * 4 + L)
        DMA("act", s_tbl.rearrange("p h k q -> p (h k q)"), ctbl_d, "tbl", reads=[], writes=[("stbl",)])
        P.add("dve", lambda e: e.memset(s_kpad.rearrange("p t k a d -> p (t k a d)"), 0.0), [], [("skpad",)])
        P.add("dve", lambda e: e.memset(s_vaug.rearrange("p t k e -> p (t k e)"), 1.0), [], [("svaug", i) for i in range(5)])
        if t > 0:
            COPY("dve", s_kTp[:, 0].rearrange("p k a s -> p (k a s)"), kcar[:, j, :], reads=[("kcar", j)], writes=[("skTp", 0)])
            COPY("dve", s_vaug[:, 0].rearrange("p k e -> p (k e)"), vcar[:, j, :], reads=[("vcar", j)], writes=[("svaug", 0)])
        for wt in range(4):
            wv, wr = load_w([W[:, wt * 512:(wt + 1) * 512]], DC)
            for nsl in range(4):
                b = lin_fm(wv, wr, DC, nsl, h_act)
                evac_copy(big[:, wt * 4 + nsl, :], ps[:, b, :], reads=[("ps", b)], writes=[("big", wt * 4 + nsl)])
        wv, wr = load_w([W[:, D:D + 512]], DC)
        for tb in range(NTB):
            b = lin_tm(wv, wr, DC, tb, 0, 512, hT, hres)
            kin = ps[:, b, 0:256].rearrange("p (k d) -> p k d", k=SKV)
            COPY("act", s_kpad[:, tb, :, 0, 0:64], kin, reads=[("ps", b)], writes=[("skpad",)])
            COPY("dve", s_kpad[:, tb, :, 1, 64:128], kin, reads=[("ps", b)], writes=[("skpad",)])
            COPY("act", s_vaug[:, tb + 1, :, 0:64], ps[:, b, 256:512].rearrange("p (k d) -> p k d", k=SKV),
                 reads=[("ps", b)], writes=[("svaug", tb + 1)])
        for tb in range(NTB):
            bb = 6 + (tb % 2)
            psb = ps[:, bb, :].bitcast(BF16)
            for kh in range(SKV):
                for ab in range(2):
                    i = kh * 2 + ab
                    TR(psb[:, i * 128:(i + 1) * 128], s_kpad[:, tb, kh, ab, :], ident[:],
                       reads=[("skpad",), ("const",)], writes=[("ps", bb)])
            evac_copy(s_kTp[:, tb + 1].rearrange("p k a s -> p (k a s)"), psb[:, 0:1024], reads=[("ps", bb)], writes=[("skTp", tb + 1)])
        otm = s_otm[0]

        def unit_scores(tb, kh, hh):
            qsl = slice(tb * 128, (tb + 1) * 128)
            kts = [1] if (t == 0 and tb == 0) else [0, 1]
            h0 = kh * 8 + hh * 4
            etiles = []
            for kt in kts:
                slot = tb + kt
                b = next_bank()
                for i in range(4):
                    h = h0 + i
                    MM(ps[:, b, i * 128:(i + 1) * 128], s_kTp[:, slot, kh, h % 2, :], big[:, h // 2, qsl], True, True,
                       reads=[("skTp", slot), ("big", h // 2)], writes=[("ps", b)])
                ei = ctr["ft"]
                ctr["ft"] = (ei + 1) % 4
                e_ = s_e[ei]
                ACT(e_, ps[:, b, :], AF.Exp, reads=[("ps", b)], writes=[("se", ei)], scale=SHD ** -0.5)
                TT_("dve", e_.rearrange("p (h q) -> p h q", h=4), e_.rearrange("p (h q) -> p h q", h=4),
                    s_tbl[:, h0:h0 + 4, kt, :], ALU.mult, reads=[("se", ei), ("stbl",)], writes=[("se", ei)])
                etiles.append((e_, ei, slot))
            return etiles

        def unit_pv(tb, kh, hh, etiles):
            h0 = kh * 8 + hh * 4
            bo = next_bank()
            for i in range(4):
                for n, (e_, ei, slot) in enumerate(etiles):
                    MM(ps[:, bo, i * 65:(i + 1) * 65], e_[:, i * 128:(i + 1) * 128], s_vaug[:, slot, kh, :],
                       n == 0, n == len(etiles) - 1, reads=[("se", ei), ("svaug", slot)], writes=[("ps", bo)])
            si = ctr["sst"]
            ctr["sst"] = (si + 1) % 4
            den = sstat[:, si, 0:4]
            pv = ps[:, bo, 0:260].rearrange("p (h e) -> p h e", e=65)
            TT_("dve", den, pv[:, :, 64], esink[:, j * SH + h0:j * SH + h0 + 4], ALU.add,
                reads=[("ps", bo), ("const2",)], writes=[("sstat", si)])
            P.add("dve", lambda e, den=den: e.reciprocal(den, den), [("sstat", si)], [("sstat", si)])
            TT_("dve", otm[:, h0 * 64:(h0 + 4) * 64].rearrange("p (h d) -> p h d", h=4), pv[:, :, 0:64],
                den.unsqueeze(2).to_broadcast([128, 4, 64]), ALU.mult,
                reads=[("ps", bo), ("sstat", si)], writes=[("sotm",)])

        def tb_finish(tb):
            qsl = slice(tb * 128, (tb + 1) * 128)
            for g in range(2):
                bb = 6 + g
                psb = ps[:, bb, :].bitcast(BF16)
                for ci in range(8):
                    c = g * 8 + ci
                    TR(psb[:, ci * 128:(ci + 1) * 128], otm[:, c * 128:(c + 1) * 128], ident[:],
                       reads=[("sotm",), ("const",)], writes=[("ps", bb)])
                evac_copy(big[:, 16 + g * 8:16 + (g + 1) * 8, qsl], psb[:, 0:1024].rearrange("p (c q) -> p c q", c=8),
                          reads=[("ps", bb)], writes=[("big", 16 + g * 8 + ci) for ci in range(8)])

        units = [(tb, kh, hh) for tb in range(NTB) for kh in range(SKV) for hh in range(2)]
        pend = unit_scores(*units[0])
        for ui, u in enumerate(units):
            nxt = unit_scores(*units[ui + 1]) if ui + 1 < len(units) else None
            unit_pv(*u, pend)
            if u[1] == SKV - 1 and u[2] == 1:
                tb_finish(u[0])
            pend = nxt
        COPY("dve", kcar[:, j, :], s_kTp[:, 4].rearrange("p k a s -> p (k a s)"), reads=[("skTp", 4)], writes=[("kcar", j)])
        COPY("dve", vcar[:, j, :], s_vaug[:, 4].rearrange("p k e -> p (k e)"), reads=[("svaug", 4)], writes=[("vcar", j)])
        Wo = swout_d[j]
        for wt in range(4):
            wv, wr = load_w([Wo[:, wt * 512:(wt + 1) * 512]], DC)
            for nsl in range(4):
                b = lin_fm(wv, wr, DC, nsl, lambda kc: (big[:, 16 + kc, :], ("big", 16 + kc)))
                resid_add(wt * 4 + nsl, b)

    def ffn_layer(t, L):
        W = fwin_d[L]
        Wo = fwout_d[L]
        rmsnorm(1 * 4 + L)
        HALF = FH // 2
        for half in range(2):
            for jt in range(11):
                c0 = half * HALF + jt * 256
                wv, wr = load_w([W[:, c0:c0 + 256], W[:, FH + c0:FH + c0 + 256]], DC)
                for jj in range(2):
                    hc = jt * 2 + jj
                    bg = lin_fm(wv, wr, DC, jj, h_act)
                    bu = lin_fm(wv, wr, DC, jj, h_act, col0=256)
                    fi = ctr["sst"] % 2
                    ctr["sst"] = (ctr["sst"] + 1) % 4
                    ACT(ftmp[:, fi, :], ps[:, bg, :], AF.Silu, reads=[("ps", bg)], writes=[("ft", fi)])
                    TT_("dve", big[:, hc, :], ftmp[:, fi, :], ps[:, bu, :], ALU.mult,
                        reads=[("ft", fi), ("ps", bu)], writes=[("big", hc)])
            for wt in range(8):
                wv, wr = load_w([Wo[half * HALF:(half + 1) * HALF, wt * 256:(wt + 1) * 256]], 22)
                for dsl in range(2):
                    b = lin_fm(wv, wr, 22, dsl, lambda kc: (big[:, kc, :], ("big", kc)))
                    resid_add(wt * 2 + dsl, b)

    def ple_layer(t, L):
        rmsnorm(2 * 4 + L)
        pst = bigf[:, 0:NTB * PLE].rearrange("p (t d) -> p t d", t=NTB)
        pbf = big[:].rearrange("p a b -> p (a b)")[:, 2048:2048 + NTB * PLE].rearrange("p (t d) -> p t d", t=NTB)
        pT = big[:, 6:8, :]
        DMA("act", pst, p_d[L, t * TT:(t + 1) * TT, :].rearrange("(tb p) d -> p tb d", p=128), "pin",
            reads=[], writes=[("big", c) for c in range(4)])
        COPY("dve", pbf, pst, reads=[("big", c) for c in range(4)], writes=[("big", 4), ("big", 5)])
        psb = ps[:, 6, :].bitcast(BF16)
        for tb in range(NTB):
            for kc in range(2):
                TR(psb[:, kc * 512 + tb * 128:kc * 512 + (tb + 1) * 128], pbf[:, tb, kc * 128:(kc + 1) * 128], ident[:],
                   reads=[("big", 4), ("big", 5), ("const",)], writes=[("ps", 6)])
        COPY("act", pT, psb[:, 0:1024].rearrange("p (c t) -> p c t", c=2), reads=[("ps", 6)], writes=[("big", 6), ("big", 7)])
        Wg = pgate_d[L]
        Wp = pproj_d[L]
        for wt in range(4):
            wv, wr = load_w([Wg[:, wt * 512:(wt + 1) * 512]], DC)
            wpv, wpr = load_w([Wp[:, wt * 512:(wt + 1) * 512]], 2)
            for nsl in range(4):
                ds = wt * 4 + nsl
                bg = lin_fm(wv, wr, DC, nsl, h_act)
                bp = lin_fm(wpv, wpr, 2, nsl, lambda kc: (pT[:, kc, :], ("big", 6 + kc)))
                fi = ctr["sst"] % 2
                ctr["sst"] = (ctr["sst"] + 1) % 4
                ACT(ftmp[:, fi, :], ps[:, bg, :], AF.Sigmoid, reads=[("ps", bg)], writes=[("ft", fi)])
                TT_("dve", ftmp[:, fi, :], ftmp[:, fi, :], ps[:, bp, :], ALU.mult, reads=[("ft", fi), ("ps", bp)], writes=[("ft", fi)])
                TT_("dve", xT[:, ds, :], xT[:, ds, :], ftmp[:, fi, :], ALU.add, reads=[("x", ds), ("ft", fi)], writes=[("x", ds)])

    load_consts()
    for t in range(NT):
        cur["t"] = t
        cur["u"] = 0
        load_x(t)
        for L in range(NL):
            if L % 2 == 0:
                ret_layer(t, L)
            else:
                swa_layer(t, L)
            ffn_layer(t, L)
            ple_layer(t, L)
        rmsnorm(12, final=True)
        store_out(t)
    counts = P.finalize()
    sems = {}
    for name in counts:
        sems[name] = es.enter_context(nc.semaphore("s_" + name))
    with nc.Block() as block:
        @block.sync
        def _(e):
            P.emit("sq", e, sems)
            e.wait_ge(sems["oout"], counts["oout"])

        @block.gpsimd
        def _(e):
            P.emit("pool", e, sems)

        @block.tensor
        def _(e):
            P.emit("pe", e, sems)

        @block.scalar
        def _(e):
            P.emit("act", e, sems)

        @block.vector
        def _(e):
            P.emit("dve", e, sems)
    es.close()
    return nc, len(P.ops)


def make_in_maps(inputs, cores):
    cst = _consts()
    shared = {
        "norm_mix": np.ascontiguousarray(inputs["norm_mix"], dtype=np.float32),
        "norm_ffn": np.ascontiguousarray(inputs["norm_ffn"], dtype=np.float32),
        "norm_ple": np.ascontiguousarray(inputs["norm_ple"], dtype=np.float32),
        "norm_final": np.ascontiguousarray(inputs["norm_final"], dtype=np.float32).reshape(1, D),
        "ret_w_in": np.ascontiguousarray(inputs["ret_w_in"], dtype=np.float32),
        "ret_gn": np.ascontiguousarray(inputs["ret_gn"], dtype=np.float32).reshape(2, RH * RDV),
        "ret_w_out": np.ascontiguousarray(inputs["ret_w_out"], dtype=np.float32),
        "swa_w_in": np.ascontiguousarray(inputs["swa_w_in"], dtype=np.float32),
        "swa_sinks": np.ascontiguousarray(inputs["swa_sinks"], dtype=np.float32).reshape(1, 2 * SH),
        "swa_w_out": np.ascontiguousarray(inputs["swa_w_out"], dtype=np.float32),
        "ffn_w_in": np.ascontiguousarray(inputs["ffn_w_in"], dtype=np.float32),
        "ffn_w_out": np.ascontiguousarray(inputs["ffn_w_out"], dtype=np.float32),
        "ple_w_proj": np.ascontiguousarray(inputs["ple_w_proj"], dtype=np.float32),
        "ple_w_gate": np.ascontiguousarray(inputs["ple_w_gate"], dtype=np.float32),
    }
    shared.update(cst)
    x = np.asarray(inputs["x"])
    p = np.asarray(inputs["p"])
    maps = []
    for b in cores:
        m = dict(shared)
        m["x"] = np.ascontiguousarray(x[b], dtype=np.float32)
        m["p"] = np.ascontiguousarray(p[:, b], dtype=np.float32)
        maps.append(m)
    return maps


def kernel(**inputs):
    nc, _ = build()
    B = np.asarray(inputs["x"]).shape[0]
    in_maps = make_in_maps(inputs, list(range(B)))
    res = run_bass_kernel_spmd(nc, in_maps, core_ids=list(range(B)))
    out = np.stack([np.asarray(r["out"], dtype=np.float32) for r in res.results], axis=0)
    return out
```

```python
import math
from contextlib import ExitStack
from functools import partial

import numpy as np
import ml_dtypes

import concourse.bass as bass
import concourse.mybir as mybir
from concourse.bass_utils import run_bass_kernel_spmd

F32 = mybir.dt.float32
BF16 = mybir.dt.bfloat16
ALU = mybir.AluOpType
AF = mybir.ActivationFunctionType

D = 2048
SEQ = 4096
DEPTH = 4
TT = 512
NTB = TT // 128
DC = D // 128
RH, RDK, RDV = 8, 256, 512
RET_IN = 12288
RVW = 4096
SH, SKV, SHD = 32, 4, 64
SWA_IN = 2560
FH = 5632
PLE = 256
EPS = 1e-6
NWS = 3
WSLOT = 8192


class Op:
    __slots__ = ("eng", "fn", "deps", "sem", "inc", "need", "cnt", "isdma")


class Prog:
    def __init__(self):
        self.ops = []
        self.last_w = {}
        self.readers = {}

    def add(self, eng, fn, reads=(), writes=(), dma_sem=None):
        op = Op()
        op.eng = eng
        op.fn = fn
        op.isdma = dma_sem is not None
        op.sem = dma_sem if op.isdma else eng
        op.inc = 16 if op.isdma else 1
        op.need = op.isdma
        op.cnt = 0
        deps = []
        lw, rd = self.last_w, self.readers
        for r in reads:
            w = lw.get(r)
            if w is not None:
                deps.append(w)
        for r in writes:
            w = lw.get(r)
            if w is not None:
                deps.append(w)
            rs = rd.get(r)
            if rs:
                deps.extend(rs)
        for r in reads:
            rd.setdefault(r, []).append(op)
        for r in writes:
            lw[r] = op
            rd[r] = []
        dd = []
        seen = set()
        for p in deps:
            if id(p) in seen:
                continue
            seen.add(id(p))
            if (not p.isdma) and (not op.isdma) and p.eng == "pe" and eng == "pe":
                continue
            p.need = True
            dd.append(p)
        op.deps = dd
        self.ops.append(op)
        return op

    def finalize(self):
        counts = {}
        for op in self.ops:
            if op.need:
                counts[op.sem] = counts.get(op.sem, 0) + op.inc
                op.cnt = counts[op.sem]
        return counts

    def emit(self, eng_name, eng, sems):
        waited = {}
        for op in self.ops:
            if op.eng != eng_name:
                continue
            need = {}
            for p in op.deps:
                if p.cnt > need.get(p.sem, 0):
                    need[p.sem] = p.cnt
            for key, cnt in need.items():
                if waited.get(key, 0) >= cnt:
                    continue
                waited[key] = cnt
                eng.wait_ge(sems[key], cnt)
            ins = op.fn(eng)
            if op.need:
                ins.then_inc(sems[op.sem], op.inc)


def _consts():
    bf = ml_dtypes.bfloat16
    c = {}
    c["c_ident"] = np.eye(128, dtype=np.float32).astype(bf)
    c["c_identf"] = np.eye(128, dtype=np.float32)
    c["c_ones"] = np.ones((128, 128), dtype=np.float32).astype(bf)
    m = np.arange(128)[:, None]
    q = np.arange(128)[None, :]
    c["c_causal"] = (q >= m).astype(np.float32).astype(bf)
    log_g = np.log1p(-np.exp2(-5.0 - np.arange(RH, dtype=np.float64)))
    pos = np.arange(128, dtype=np.float64)
    qdec = np.exp(log_g[:, None] * (pos[None, :] + 1.0))
    c["c_qdec"] = np.broadcast_to(qdec.reshape(1, RH * 128), (128, RH * 128)).astype(np.float32).copy()
    kdec = (RDK ** -0.5) * np.exp(log_g[None, :] * (127.0 - pos[:, None]))
    c["c_kdec"] = kdec.astype(np.float32).copy()
    slopes = np.exp2(-8.0 * (np.arange(SH, dtype=np.float64) + 1.0) / SH)
    s = np.arange(128)[:, None]
    qq = np.arange(128)[None, :]
    tbl = np.zeros((128, SH, 2, 128), dtype=np.float64)
    rel_prev = (qq - s + 128).astype(np.float64)
    rel_cur = (qq - s).astype(np.float64)
    for h in range(SH):
        tbl[:, h, 0, :] = np.where(qq < s, np.exp(-slopes[h] * rel_prev), 0.0)
        tbl[:, h, 1, :] = np.where(qq >= s, np.exp(-slopes[h] * rel_cur), 0.0)
    c["c_swatbl"] = tbl.reshape(128, SH * 2 * 128).astype(np.float32).astype(bf)
    return c


RET_GAMMA = [1.0 - 2.0 ** (-5.0 - h) for h in range(RH)]
RET_CHUNK_DECAY = [float(np.float32(np.exp(np.float64(128.0) * np.log1p(-np.exp2(-5.0 - h))))) for h in range(RH)]
RET_INV_CD = [float(np.exp(-np.float64(128.0) * np.log1p(-np.exp2(-5.0 - h)))) for h in range(RH)]


def build(NT=SEQ // TT, NL=DEPTH):
    nc = bass.Bass("TRN2", target_bir_lowering=False)
    P = Prog()
    es = ExitStack()

    def din(name, shape, dt=F32):
        return nc.dram_tensor(name, list(shape), dt, kind="ExternalInput").ap()

    x_d = din("x", [SEQ, D])
    p_d = din("p", [DEPTH, SEQ, PLE])
    nmix_d = din("norm_mix", [DEPTH, D])
    nffn_d = din("norm_ffn", [DEPTH, D])
    nple_d = din("norm_ple", [DEPTH, D])
    nfin_d = din("norm_final", [1, D])
    rwin_d = din("ret_w_in", [2, D, RET_IN])
    rgn_d = din("ret_gn", [2, RH * RDV])
    rwout_d = din("ret_w_out", [2, RVW, D])
    swin_d = din("swa_w_in", [2, D, SWA_IN])
    ssink_d = din("swa_sinks", [1, 2 * SH])
    swout_d = din("swa_w_out", [2, D, D])
    fwin_d = din("ffn_w_in", [DEPTH, D, 2 * FH])
    fwout_d = din("ffn_w_out", [DEPTH, FH, D])
    pproj_d = din("ple_w_proj", [DEPTH, PLE, D])
    pgate_d = din("ple_w_gate", [DEPTH, D, D])
    cid_d = din("c_ident", [128, 128], BF16)
    cidf_d = din("c_identf", [128, 128])
    cones_d = din("c_ones", [128, 128], BF16)
    ccaus_d = din("c_causal", [128, 128], BF16)
    cqdec_d = din("c_qdec", [128, RH * 128])
    ckdec_d = din("c_kdec", [128, RH])
    ctbl_d = din("c_swatbl", [128, SH * 2 * 128], BF16)
    out_d = nc.dram_tensor("out", [SEQ, D], F32, kind="ExternalOutput").ap()
    st_d = nc.dram_tensor("state_scratch", [2 * RH * 128, 2 * RDV], F32, kind="Internal").ap()
    WCAP = 1000000
    wbf_ds = [nc.dram_tensor("wbf_scratch%d" % i, [128, WCAP], BF16, kind="Internal").ap() for i in range(2)]
    uoff = []
    walloc = {"i": 0, "off": 0}

    def sb(name, shape, dt):
        return es.enter_context(nc.sbuf_tensor(name, list(shape), dt))

    xT = sb("xT", [128, DC, TT], F32)
    hT = sb("hT", [128, DC, TT], BF16)
    big = sb("big", [128, 32, TT], BF16)
    wsl = sb("wsl", [128, NWS, WSLOT], BF16)
    mixb = sb("mixb", [128, 23040], BF16)
    mixf = sb("mixf", [128, 2, 2 * RDV], F32)
    gains = sb("gains", [128, 13, DC], F32)
    ident = sb("ident", [128, 128], BF16)
    identf = sb("identf", [128, 128], F32)
    ones = sb("ones", [128, 128], BF16)
    causal = sb("causal", [128, 128], BF16)
    qdec = sb("qdec", [128, RH, 128], F32)
    kdec = sb("kdec", [128, RH], F32)
    esink = sb("esink", [128, 2 * SH], F32)
    sq = sb("sq", [128, 2, TT], BF16)
    rstd = sb("rstd", [128, TT], F32)
    ftmp = sb("ftmp", [128, 2, TT], F32)
    kcar = sb("kcar", [128, 2, SKV * 2 * 128], BF16)
    vcar = sb("vcar", [128, 2, SKV * 65], BF16)
    stat = sb("stat", [128, 4, 16], F32)
    sstat = sb("sstat", [128, 4, 8], F32)
    epsb = sb("epsb", [128, 8], F32)
    ps = es.enter_context(nc.psum_tensor("ps", [128, 8, 512], F32))

    bigf = big[:].rearrange("p a b -> p (a b)").bitcast(F32)
    stage = bigf.rearrange("p (t d) -> p t d", t=NTB)

    off = [0]

    def carve(n, shape_str=None, **kw):
        a = mixb[:, off[0]:off[0] + n]
        off[0] += n
        if shape_str:
            a = a.rearrange(shape_str, **kw)
        return a

    r_qT = [carve(2 * TT, "p (c t) -> p c t", c=2) for _ in range(2)]
    r_kpp = [carve(NTB * RDK, "p (t d) -> p t d", t=NTB) for _ in range(2)]
    r_v = [carve(NTB * RDV, "p (t e) -> p t e", t=NTB) for _ in range(2)]
    r_sg = [carve(NTB * RDV, "p (t e) -> p t e", t=NTB) for _ in range(2)]
    r_kT = [carve(2 * 128, "p (c t) -> p c t", c=2) for _ in range(2)]
    r_sT = [carve(128) for _ in range(2)]
    r_Sbf = [carve(2 * RDV, "p (c e) -> p c e", c=2) for _ in range(2)]
    r_yn = [carve(RDV) for _ in range(2)]
    r_z = [carve(RDV) for _ in range(2)]
    r_gn = carve(RH * RDV, "p (h e) -> p h e", h=RH)
    ret_end = off[0]
    off[0] = 0
    s_tbl = carve(SH * 2 * 128, "p (h k q) -> p h k q", h=SH, k=2)
    s_kpad = carve(NTB * SKV * 2 * 128, "p (t k a d) -> p t k a d", t=NTB, k=SKV, a=2)
    s_kTp = carve(5 * SKV * 2 * 128, "p (t k a s) -> p t k a s", t=5, k=SKV, a=2)
    s_vaug = carve(5 * SKV * 65, "p (t k e) -> p t k e", t=5, k=SKV)
    s_e = [carve(512) for _ in range(4)]
    s_otm = [carve(D) for _ in range(1)]
    swa_end = off[0]
    assert max(ret_end, swa_end) <= 23040, (ret_end, swa_end)

    ctr = {"bank": 0, "ws": 0, "sq": 0, "ft": 0, "alt": 0, "st": 0, "sst": 0}

    def next_bank():
        b = ctr["bank"]
        ctr["bank"] = (b + 1) % 6
        return b

    def alt_engine():
        ctr["alt"] ^= 1
        return "act" if ctr["alt"] else "dve"

    def MM(out, lhsT, rhs, start, stop, reads, writes):
        P.add("pe", lambda e: e.matmul(out, lhsT, rhs, start=start, stop=stop), reads, writes)

    def TR(out, in_, idt, reads, writes):
        P.add("pe", lambda e: e.transpose(out, in_, idt), reads, writes)

    def ACT(out, in_, func, reads, writes, bias=None, scale=None):
        kw = {}
        if bias is not None:
            kw["bias"] = bias
        if scale is not None:
            kw["scale"] = scale
        P.add("act", lambda e: e.activation(out, in_, func, **kw), reads, writes)

    def TT_(eng, out, in0, in1, op, reads, writes):
        P.add(eng, lambda e: e.tensor_tensor(out, in0, in1, op), reads, writes)

    def TS(eng, out, in0, s1, s2, op0, op1, reads, writes):
        if op1 is None:
            P.add(eng, lambda e: e.tensor_scalar(out, in0, s1, None, op0), reads, writes)
        else:
            P.add(eng, lambda e: e.tensor_scalar(out, in0, s1, s2, op0, op1), reads, writes)

    def STT(eng, out, in0, scalar, in1, op0, op1, reads, writes):
        P.add(eng, lambda e: e.scalar_tensor_tensor(out, in0, scalar, in1, op0, op1), reads, writes)

    def COPY(eng, out, in_, reads, writes):
        if eng == "act":
            P.add("act", lambda e: e.copy(out, in_), reads, writes)
        else:
            P.add(eng, lambda e: e.tensor_copy(out, in_), reads, writes)

    def DMA(q, out, in_, sem, reads, writes, slow=False):
        if slow:
            P.add(q, lambda e: e.dma_start(out=out, in_=in_, allow_slow_non_contiguous=True), reads, writes, dma_sem=sem)
        else:
            P.add(q, lambda e: e.dma_start(out=out, in_=in_), reads, writes, dma_sem=sem)

    def evac_copy(out, in_, reads, writes):
        COPY(alt_engine(), out, in_, reads, writes)

    cur = {"t": 0, "u": 0, "L": 0}
    CVT_TILE = {0: 1, 1: 0, 2: 2, 3: 2}

    def load_w(parts, KC):
        s = ctr["ws"]
        ctr["ws"] = (s + 1) % NWS
        u = cur["u"]
        cur["u"] += 1
        ntot = sum(pp.shape[1] for pp in parts)
        nel = KC * ntot
        assert nel <= WSLOT, (KC, ntot)
        view = wsl[:, s, 0:nel].rearrange("p (k n) -> p k n", k=KC)
        if cur["t"] == 0:
            if walloc["off"] + nel > WCAP:
                walloc["i"] += 1
                walloc["off"] = 0
            assert walloc["i"] < len(wbf_ds)
            uoff.append((walloc["i"], walloc["off"]))
            walloc["off"] += nel
        wi, wo = uoff[u]
        scr = wbf_ds[wi][:, wo:wo + nel]
        cvt = CVT_TILE[cur["L"]]
        if cur["t"] <= cvt:
            o = 0
            for pp in parts:
                n = pp.shape[1]
                DMA("pool", view[:, :, o:o + n], pp.rearrange("(k p) n -> p k n", p=128), "w%d" % s,
                    reads=[], writes=[("w", s)])
                o += n
            if cur["t"] == cvt and NT > cvt + 1:
                DMA("sq", scr, wsl[:, s, 0:KC * ntot], "wb%d" % s, reads=[("w", s)], writes=[("wbf", u)])
        else:
            DMA("sq", wsl[:, s, 0:KC * ntot], scr, "v%d" % s, reads=[("wbf", u)], writes=[("w", s)])
        return view, ("w", s)

    def lin_fm(wview, wres, KC, nsl, act_fn, col0=0):
        b = next_bank()
        for kc in range(KC):
            a, ares = act_fn(kc)
            MM(ps[:, b, :], wview[:, kc, col0 + nsl * 128:col0 + (nsl + 1) * 128], a, kc == 0, kc == KC - 1,
               reads=[wres, ares], writes=[("ps", b)])
        return b

    def lin_fm_g(out, wview, wres, KC, nsl, act_fn, col0=0):
        b = next_bank()
        out.append(b)
        for kc in range(KC):
            a, ares = act_fn(kc)
            MM(ps[:, b, :], wview[:, kc, col0 + nsl * 128:col0 + (nsl + 1) * 128], a, kc == 0, kc == KC - 1,
               reads=[wres, ares], writes=[("ps", b)])
            if kc == KC // 2 - 1:
                yield

    def lin_tm_g(out, wview, wres, KC, tb, c0, n, act3, actres_fn):
        b = next_bank()
        out.append(b)
        for kc in range(KC):
            MM(ps[:, b, 0:n], act3[:, kc, tb * 128:(tb + 1) * 128], wview[:, kc, c0:c0 + n], kc == 0, kc == KC - 1,
               reads=[wres, actres_fn(kc)], writes=[("ps", b)])
            if kc == KC // 2 - 1:
                yield

    def lin_tm(wview, wres, KC, tb, c0, n, act3, actres_fn):
        b = next_bank()
        for kc in range(KC):
            MM(ps[:, b, 0:n], act3[:, kc, tb * 128:(tb + 1) * 128], wview[:, kc, c0:c0 + n], kc == 0, kc == KC - 1,
               reads=[wres, actres_fn(kc)], writes=[("ps", b)])
        return b

    def hres(kc):
        return ("h", kc)

    def h_act(kc):
        return hT[:, kc, :], ("h", kc)

    def rmsnorm(grow, final=False):
        b = next_bank()
        for c in range(DC):
            i = ctr["sq"]
            ctr["sq"] ^= 1
            ACT(sq[:, i, :], xT[:, c, :], AF.Square, reads=[("x", c)], writes=[("sq", i)])
            MM(ps[:, b, :], ones[:], sq[:, i, :], c == 0, c == DC - 1, reads=[("sq", i), ("const",)], writes=[("ps", b)])
        ACT(rstd[:], ps[:, b, :], AF.Sqrt, reads=[("ps", b), ("eps",)], writes=[("rstd",)], bias=epsb[:, 0:1], scale=1.0 / D)
        P.add("dve", lambda e: e.reciprocal(rstd[:], rstd[:]), [("rstd",)], [("rstd",)])
        for c in range(DC):
            if final:
                STT("dve", xT[:, c, :], xT[:, c, :], gains[:, grow, c:c + 1], rstd[:], ALU.mult, ALU.mult,
                    reads=[("x", c), ("rstd",), ("const",)], writes=[("x", c)])
            else:
                STT("dve", hT[:, c, :], xT[:, c, :], gains[:, grow, c:c + 1], rstd[:], ALU.mult, ALU.mult,
                    reads=[("x", c), ("rstd",), ("const",)], writes=[("h", c)])

    def resid_add(ds, b):
        TT_("dve", xT[:, ds, :], xT[:, ds, :], ps[:, b, :], ALU.add, reads=[("x", ds), ("ps", b)], writes=[("x", ds)])

    def load_consts():
        cl = []

        def CD(out, in_, slow=False):
            r = ("cst", len(cl))
            cl.append(r)
            DMA("sq", out, in_, "const", [], [r], slow=slow)

        CD(ident[:], cid_d)
        CD(identf[:], cidf_d)
        CD(ones[:], cones_d)
        CD(causal[:], ccaus_d)
        CD(qdec[:].rearrange("p h c -> p (h c)"), cqdec_d)
        CD(kdec[:], ckdec_d)
        for i, g in enumerate((nmix_d, nffn_d, nple_d)):
            for l in range(DEPTH):
                CD(gains[:, i * 4 + l, :], g[l:l + 1, :].rearrange("o (c p) -> p (o c)", p=128), slow=True)
        CD(gains[:, 12, :], nfin_d.rearrange("o (c p) -> p (o c)", p=128), slow=True)
        CD(esink[:], ssink_d.partition_broadcast(128))
        P.add("dve", lambda e: e.memset(epsb[:], EPS), cl, [("eps",), ("const",)])
        ACT(esink[:], esink[:], AF.Exp, reads=[("const",)], writes=[("const2",)])

    def load_x(t):
        DMA("act", stage, x_d[t * TT:(t + 1) * TT, :].rearrange("(tb p) d -> p tb d", p=128), "xin",
            reads=[], writes=[("big", c) for c in range(32)])
        for c in range(DC):
            b = next_bank()
            for tb in range(NTB):
                TR(ps[:, b, tb * 128:(tb + 1) * 128], stage[:, tb, c * 128:(c + 1) * 128], identf[:],
                   reads=[("big", tb * 8 + c // 2), ("const",)], writes=[("ps", b)])
            evac_copy(xT[:, c, :], ps[:, b, :], reads=[("ps", b)], writes=[("x", c)])

    def store_out(t):
        for tb in range(NTB):
            for cg in range(4):
                b = next_bank()
                for ci in range(4):
                    c = cg * 4 + ci
                    TR(ps[:, b, ci * 128:(ci + 1) * 128], xT[:, c, tb * 128:(tb + 1) * 128], identf[:],
                       reads=[("x", c), ("const",)], writes=[("ps", b)])
                evac_copy(stage[:, tb, cg * 512:(cg + 1) * 512], ps[:, b, :], reads=[("ps", b)],
                          writes=[("big", tb * 8 + cg * 2), ("big", tb * 8 + cg * 2 + 1)])
            DMA("act", out_d[t * TT + tb * 128:t * TT + (tb + 1) * 128, :], stage[:, tb, :], "oout",
                reads=[("big", tb * 8 + i) for i in range(8)], writes=[("outd",)])

    def ret_proj(t, L, h):
        j = L // 2
        W = rwin_d[j]
        hb = h % 2
        qT, kpp, v, sg = r_qT[hb], r_kpp[hb], r_v[hb], r_sg[hb]
        S = mixf[:, hb, :].rearrange("p (c e) -> p c e", c=2)
        srow = (j * RH + h) * 128
        if t == 0:
            P.add("dve", lambda e, S=S: e.memset(S, 0.0), [], [("S", hb)])
        else:
            DMA("act", S.rearrange("p c e -> p (c e)"), st_d[srow:srow + 128, :], "sld%d" % hb,
                reads=[("std", j, h)], writes=[("S", hb)])
        wv, wr = load_w([W[:, h * RDK:(h + 1) * RDK], W[:, D + h * RDK:D + (h + 1) * RDK]], DC)
        for dc in range(2):
            bl = []
            yield from lin_fm_g(bl, wv, wr, DC, dc, h_act)
            b = bl[0]
            TT_("dve", qT[:, dc, :].rearrange("p (t c) -> p t c", t=NTB), ps[:, b, :].rearrange("p (t c) -> p t c", t=NTB),
                qdec[:, h, :].unsqueeze(1).to_broadcast([128, NTB, 128]), ALU.mult,
                reads=[("ps", b), ("const",)], writes=[("rq", hb)])
            yield
        for tb in range(NTB):
            bl = []
            yield from lin_tm_g(bl, wv, wr, DC, tb, 256, 256, hT, hres)
            b = bl[0]
            ACT(kpp[:, tb, :], ps[:, b, 0:256], AF.Identity, reads=[("ps", b), ("const",)], writes=[("rk", hb)],
                scale=kdec[:, h:h + 1])
            yield
        wv, wr = load_w([W[:, 2 * D + h * RDV:2 * D + (h + 1) * RDV]], DC)
        for tb in range(NTB):
            bl = []
            yield from lin_tm_g(bl, wv, wr, DC, tb, 0, 512, hT, hres)
            b = bl[0]
            evac_copy(v[:, tb, :], ps[:, b, :], reads=[("ps", b)], writes=[("rv", hb)])
            yield
        wv, wr = load_w([W[:, 2 * D + RVW + h * RDV:2 * D + RVW + (h + 1) * RDV]], DC)
        for tb in range(NTB):
            bl = []
            yield from lin_tm_g(bl, wv, wr, DC, tb, 0, 512, hT, hres)
            b = bl[0]
            ACT(sg[:, tb, :], ps[:, b, :], AF.Silu, reads=[("ps", b)], writes=[("rsg", hb)])
            yield
        TT_("dve", sg, sg, r_gn[:, h, :].unsqueeze(1).to_broadcast([128, NTB, RDV]), ALU.mult,
            reads=[("rsg", hb), ("gn",)], writes=[("rsg", hb)])

    def ret_chunks(t, L, h):
        j = L // 2
        hb = h % 2
        qT, kpp, v, sg = r_qT[hb], r_kpp[hb], r_v[hb], r_sg[hb]
        S = mixf[:, hb, :].rearrange("p (c e) -> p c e", c=2)
        srow = (j * RH + h) * 128
        COPY("act", r_Sbf[0], S, reads=[("S", hb)], writes=[("Sbf", 0)])
        for cb in range(NTB):
            par = cb % 2
            kT = r_kT[par]
            Sb_cur = r_Sbf[cb % 2]
            Sb_nxt = r_Sbf[(cb + 1) % 2]
            csl = slice(cb * 128, (cb + 1) * 128)
            psb = ps[:, 6, :].bitcast(BF16)
            for dc in range(2):
                TR(psb[:, dc * 128:(dc + 1) * 128], kpp[:, cb, dc * 128:(dc + 1) * 128], ident[:],
                   reads=[("rk", hb), ("const",)], writes=[("ps", 6)])
            COPY("dve", kT.rearrange("p c t -> p (c t)"), psb[:, 0:256], reads=[("ps", 6)], writes=[("rkT", par)])
            bus = []
            for dc in range(2):
                bu = next_bank()
                bus.append(bu)
                MM(ps[:, bu, :], kpp[:, cb, dc * 128:(dc + 1) * 128], v[:, cb, :], True, True,
                   reads=[("rk", hb), ("rv", hb)], writes=[("ps", bu)])
            yield
            b = next_bank()
            for dc in range(2):
                MM(ps[:, b, 0:128], kT[:, dc, :], qT[:, dc, csl], dc == 0, dc == 1,
                   reads=[("rkT", par), ("rq", hb)], writes=[("ps", b)])
            sT = r_sT[par]
            STT("dve", sT, ps[:, b, 0:128], RET_INV_CD[h], causal[:], ALU.mult, ALU.mult,
                reads=[("ps", b), ("const",)], writes=[("rsT", par)])
            yield
            bo = next_bank()
            for dc in range(2):
                MM(ps[:, bo, :], qT[:, dc, csl], Sb_cur[:, dc, :], dc == 0, False,
                   reads=[("rq", hb), ("Sbf", cb % 2)], writes=[("ps", bo)])
            MM(ps[:, bo, :], sT, v[:, cb, :], False, True, reads=[("rsT", par), ("rv", hb)], writes=[("ps", bo)])
            for dc in range(2):
                STT("dve", S[:, dc, :], S[:, dc, :], RET_CHUNK_DECAY[h], ps[:, bus[dc], :], ALU.mult, ALU.add,
                    reads=[("S", hb), ("ps", bus[dc])], writes=[("S", hb)])
            if cb < NTB - 1:
                COPY("act", Sb_nxt, S, reads=[("S", hb)], writes=[("Sbf", (cb + 1) % 2)])
            yield
            si = ctr["st"]
            ctr["st"] = (si + 1) % 4
            st6 = stat[:, si, 0:6]
            mv = stat[:, si, 6:8]
            rs = stat[:, si, 8:9]
            nb = stat[:, si, 9:10]
            P.add("dve", lambda e, st6=st6, bo=bo: e.bn_stats(st6, ps[:, bo, :]), [("ps", bo)], [("stat", si)])
            P.add("dve", lambda e, st6=st6, mv=mv: e.bn_aggr(mv, st6), [("stat", si)], [("stat", si)])
            ACT(rs, mv[:, 1:2], AF.Sqrt, reads=[("stat", si), ("eps",)], writes=[("stat", si)], bias=epsb[:, 0:1], scale=1.0)
            P.add("dve", lambda e, rs=rs: e.reciprocal(rs, rs), [("stat", si)], [("stat", si)])
            STT("dve", nb, mv[:, 0:1], -1.0, rs, ALU.mult, ALU.mult, reads=[("stat", si)], writes=[("stat", si)])
            yn = r_yn[par]
            ACT(yn, ps[:, bo, :], AF.Identity, reads=[("ps", bo), ("stat", si)], writes=[("ryn", par)], bias=nb, scale=rs)
            z = r_z[par]
            TT_("dve", z, yn, sg[:, cb, :], ALU.mult, reads=[("ryn", par), ("rsg", hb)], writes=[("rz", par)])
            yield
            psb7 = ps[:, 7, :].bitcast(BF16)
            for ec in range(4):
                TR(psb7[:, ec * 128:(ec + 1) * 128], z[:, ec * 128:(ec + 1) * 128], ident[:],
                   reads=[("rz", par), ("const",)], writes=[("ps", 7)])
            COPY("act", big[:, h * 4:(h + 1) * 4, csl], psb7[:, 0:512].rearrange("p (e t) -> p e t", e=4),
                 reads=[("ps", 7)], writes=[("big", h * 4 + ec) for ec in range(4)])
            yield
        DMA("act", st_d[srow:srow + 128, :], S.rearrange("p c e -> p (c e)"), "sst%d" % hb,
            reads=[("S", hb)], writes=[("std", j, h)])

    def ret_layer(t, L):
        j = L // 2
        rmsnorm(0 * 4 + L)
        alias = [("skTp", i) for i in range(5)] + [("svaug", i) for i in range(5)] + [("se", i) for i in range(4)] + [("sotm",), ("skpad",), ("stbl",)]
        DMA("pool", r_gn.rearrange("p h e -> p (h e)"), rgn_d[j:j + 1, :].partition_broadcast(128), "gn",
            reads=[], writes=[("gn",)] + alias)
        for _ in ret_proj(t, L, 0):
            pass
        for h in range(RH):
            A = ret_chunks(t, L, h)
            B = ret_proj(t, L, h + 1) if h + 1 < RH else None
            a_done = False
            b_done = B is None
            while not (a_done and b_done):
                if not a_done:
                    try:
                        next(A)
                    except StopIteration:
                        a_done = True
                if not b_done:
                    try:
                        next(B)
                    except StopIteration:
                        b_done = True
        Wo = rwout_d[j]
        for wt in range(8):
            wv, wr = load_w([Wo[:, wt * 256:(wt + 1) * 256]], 32)
            for dsl in range(2):
                b = lin_fm(wv, wr, 32, dsl, lambda kc: (big[:, kc, :], ("big", kc)))
                resid_add(wt * 2 + dsl, b)

    def swa_layer(t, L):
        j = L // 2
        W = swin_d[j]
        rmsnorm(0 * 4 + L)
        DMA("act", s_tbl.rearrange("p h k q -> p (h k q)"), ctbl_d, "tbl", reads=[], writes=[("stbl",)])
        P.add("dve", lambda e: e.memset(s_kpad.rearrange("p t k a d -> p (t k a d)"), 0.0), [], [("skpad",)])
        P.add("dve", lambda e: e.memset(s_vaug.rearrange("p t k e -> p (t k e)"), 1.0), [], [("svaug", i) for i in range(5)])
        if t > 0:
            COPY("dve", s_kTp[:, 0].rearrange("p k a s -> p (k a s)"), kcar[:, j, :], reads=[("kcar", j)], writes=[("skTp", 0)])
            COPY("dve", s_vaug[:, 0].rearrange("p k e -> p (k e)"), vcar[:, j, :], reads=[("vcar", j)], writes=[("svaug", 0)])
        for wt in range(4):
            wv, wr = load_w([W[:, wt * 512:(wt + 1) * 512]], DC)
            for nsl in range(4):
                b = lin_fm(wv, wr, DC, nsl, h_act)
                evac_copy(big[:, wt * 4 + nsl, :], ps[:, b, :], reads=[("ps", b)], writes=[("big", wt * 4 + nsl)])
        wv, wr = load_w([W[:, D:D + 512]], DC)
        for tb in range(NTB):
            b = lin_tm(wv, wr, DC, tb, 0, 512, hT, hres)
            kin = ps[:, b, 0:256].rearrange("p (k d) -> p k d", k=SKV)
            COPY("act", s_kpad[:, tb, :, 0, 0:64], kin, reads=[("ps", b)], writes=[("skpad",)])
            COPY("dve", s_kpad[:, tb, :, 1, 64:128], kin, reads=[("ps", b)], writes=[("skpad",)])
            COPY("act", s_vaug[:, tb + 1, :, 0:64], ps[:, b, 256:512].rearrange("p (k d) -> p k d", k=SKV),
                 reads=[("ps", b)], writes=[("svaug", tb + 1)])
        for tb in range(NTB):
            bb = 6 + (tb % 2)
            psb = ps[:, bb, :].bitcast(BF16)
            for kh in range(SKV):
                for ab in range(2):
                    i = kh * 2 + ab
                    TR(psb[:, i * 128:(i + 1) * 128], s_kpad[:, tb, kh, ab, :], ident[:],
                       reads=[("skpad",), ("const",)], writes=[("ps", bb)])
            evac_copy(s_kTp[:, tb + 1].rearrange("p k a s -> p (k a s)"), psb[:, 0:1024], reads=[("ps", bb)], writes=[("skTp", tb + 1)])
        otm = s_otm[0]

        def unit_scores(tb, kh, hh):
            qsl = slice(tb * 128, (tb + 1) * 128)
            kts = [1] if (t == 0 and tb == 0) else [0, 1]
            h0 = kh * 8 + hh * 4
            etiles = []
            for kt in kts:
                slot = tb + kt
                b = next_bank()
                for i in range(4):
                    h = h0 + i
                    MM(ps[:, b, i * 128:(i + 1) * 128], s_kTp[:, slot, kh, h % 2, :], big[:, h // 2, qsl], True, True,
                       reads=[("skTp", slot), ("big", h // 2)], writes=[("ps", b)])
                ei = ctr["ft"]
                ctr["ft"] = (ei + 1) % 4
                e_ = s_e[ei]
                ACT(e_, ps[:, b, :], AF.Exp, reads=[("ps", b)], writes=[("se", ei)], scale=SHD ** -0.5)
                TT_("dve", e_.rearrange("p (h q) -> p h q", h=4), e_.rearrange("p (h q) -> p h q", h=4),
                    s_tbl[:, h0:h0 + 4, kt, :], ALU.mult, reads=[("se", ei), ("stbl",)], writes=[("se", ei)])
                etiles.append((e_, ei, slot))
            return etiles

        def unit_pv(tb, kh, hh, etiles):
            h0 = kh * 8 + hh * 4
            bo = next_bank()
            for i in range(4):
                for n, (e_, ei, slot) in enumerate(etiles):
                    MM(ps[:, bo, i * 65:(i + 1) * 65], e_[:, i * 128:(i + 1) * 128], s_vaug[:, slot, kh, :],
                       n == 0, n == len(etiles) - 1, reads=[("se", ei), ("svaug", slot)], writes=[("ps", bo)])
            si = ctr["sst"]
            ctr["sst"] = (si + 1) % 4
            den = sstat[:, si, 0:4]
            pv = ps[:, bo, 0:260].rearrange("p (h e) -> p h e", e=65)
            TT_("dve", den, pv[:, :, 64], esink[:, j * SH + h0:j * SH + h0 + 4], ALU.add,
                reads=[("ps", bo), ("const2",)], writes=[("sstat", si)])
            P.add("dve", lambda e, den=den: e.reciprocal(den, den), [("sstat", si)], [("sstat", si)])
            TT_("dve", otm[:, h0 * 64:(h0 + 4) * 64].rearrange("p (h d) -> p h d", h=4), pv[:, :, 0:64],
                den.unsqueeze(2).to_broadcast([128, 4, 64]), ALU.mult,
                reads=[("ps", bo), ("sstat", si)], writes=[("sotm",)])

        def tb_finish(tb):
            qsl = slice(tb * 128, (tb + 1) * 128)
            for g in range(2):
                bb = 6 + g
                psb = ps[:, bb, :].bitcast(BF16)
                for ci in range(8):
                    c = g * 8 + ci
                    TR(psb[:, ci * 128:(ci + 1) * 128], otm[:, c * 128:(c + 1) * 128], ident[:],
                       reads=[("sotm",), ("const",)], writes=[("ps", bb)])
                evac_copy(big[:, 16 + g * 8:16 + (g + 1) * 8, qsl], psb[:, 0:1024].rearrange("p (c q) -> p c q", c=8),
                          reads=[("ps", bb)], writes=[("big", 16 + g * 8 + ci) for ci in range(8)])

        units = [(tb, kh, hh) for tb in range(NTB) for kh in range(SKV) for hh in range(2)]
        pend = unit_scores(*units[0])
        for ui, u in enumerate(units):
            nxt = unit_scores(*units[ui + 1]) if ui + 1 < len(units) else None
            unit_pv(*u, pend)
            if u[1] == SKV - 1 and u[2] == 1:
                tb_finish(u[0])
            pend = nxt
        COPY("dve", kcar[:, j, :], s_kTp[:, 4].rearrange("p k a s -> p (k a s)"), reads=[("skTp", 4)], writes=[("kcar", j)])
        COPY("dve", vcar[:, j, :], s_vaug[:, 4].rearrange("p k e -> p (k e)"), reads=[("svaug", 4)], writes=[("vcar", j)])
        Wo = swout_d[j]
        for wt in range(4):
            wv, wr = load_w([Wo[:, wt * 512:(wt + 1) * 512]], DC)
            for nsl in range(4):
                b = lin_fm(wv, wr, DC, nsl, lambda kc: (big[:, 16 + kc, :], ("big", 16 + kc)))
                resid_add(wt * 4 + nsl, b)

    def ffn_layer(t, L):
        W = fwin_d[L]
        Wo = fwout_d[L]
        rmsnorm(1 * 4 + L)
        HALF = FH // 2
        for half in range(2):
            for jt in range(11):
                c0 = half * HALF + jt * 256
                wv, wr = load_w([W[:, c0:c0 + 256], W[:, FH + c0:FH + c0 + 256]], DC)
                for jj in range(2):
                    hc = jt * 2 + jj
                    bg = lin_fm(wv, wr, DC, jj, h_act)
                    bu = lin_fm(wv, wr, DC, jj, h_act, col0=256)
                    fi = ctr["sst"] % 2
                    ctr["sst"] = (ctr["sst"] + 1) % 4
                    ACT(ftmp[:, fi, :], ps[:, bg, :], AF.Silu, reads=[("ps", bg)], writes=[("ft", fi)])
                    TT_("dve", big[:, hc, :], ftmp[:, fi, :], ps[:, bu, :], ALU.mult,
                        reads=[("ft", fi), ("ps", bu)], writes=[("big", hc)])
            for wt in range(8):
                wv, wr = load_w([Wo[half * HALF:(half + 1) * HALF, wt * 256:(wt + 1) * 256]], 22)
                for dsl in range(2):
                    b = lin_fm(wv, wr, 22, dsl, lambda kc: (big[:, kc, :], ("big", kc)))
                    resid_add(wt * 2 + dsl, b)

    def ple_layer(t, L):
        rmsnorm(2 * 4 + L)
        pst = bigf[:, 0:NTB * PLE].rearrange("p (t d) -> p t d", t=NTB)
        pbf = big[:].rearrange("p a b -> p (a b)")[:, 2048:2048 + NTB * PLE].rearrange("p (t d) -> p t d", t=NTB)
        pT = big[:, 6:8, :]
        DMA("act", pst, p_d[L, t * TT:(t + 1) * TT, :].rearrange("(tb p) d -> p tb d", p=128), "pin",
            reads=[], writes=[("big", c) for c in range(4)])
        COPY("dve", pbf, pst, reads=[("big", c) for c in range(4)], writes=[("big", 4), ("big", 5)])
        psb = ps[:, 6, :].bitcast(BF16)
        for tb in range(NTB):
            for kc in range(2):
                TR(psb[:, kc * 512 + tb * 128:kc * 512 + (tb + 1) * 128], pbf[:, tb, kc * 128:(kc + 1) * 128], ident[:],
                   reads=[("big", 4), ("big", 5), ("const",)], writes=[("ps", 6)])
        COPY("act", pT, psb[:, 0:1024].rearrange("p (c t) -> p c t", c=2), reads=[("ps", 6)], writes=[("big", 6), ("big", 7)])
        Wg = pgate_d[L]
        Wp = pproj_d[L]
        for wt in range(4):
            wv, wr = load_w([Wg[:, wt * 512:(wt + 1) * 512]], DC)
            wpv, wpr = load_w([Wp[:, wt * 512:(wt + 1) * 512]], 2)
            for nsl in range(4):
                ds = wt * 4 + nsl
                bg = lin_fm(wv, wr, DC, nsl, h_act)
                bp = lin_fm(wpv, wpr, 2, nsl, lambda kc: (pT[:, kc, :], ("big", 6 + kc)))
                fi = ctr["sst"] % 2
                ctr["sst"] = (ctr["sst"] + 1) % 4
                ACT(ftmp[:, fi, :], ps[:, bg, :], AF.Sigmoid, reads=[("ps", bg)], writes=[("ft", fi)])
                TT_("dve", ftmp[:, fi, :], ftmp[:, fi, :], ps[:, bp, :], ALU.mult, reads=[("ft", fi), ("ps", bp)], writes=[("ft", fi)])
                TT_("dve", xT[:, ds, :], xT[:, ds, :], ftmp[:, fi, :], ALU.add, reads=[("x", ds), ("ft", fi)], writes=[("x", ds)])

    load_consts()
    for t in range(NT):
        cur["t"] = t
        cur["u"] = 0
        load_x(t)
        for L in range(NL):
            cur["L"] = L
            if L % 2 == 0:
                ret_layer(t, L)
            else:
                swa_layer(t, L)
            ffn_layer(t, L)
            ple_layer(t, L)
        rmsnorm(12, final=True)
        store_out(t)
    counts = P.finalize()
    sems = {}
    for name in counts:
        sems[name] = es.enter_context(nc.semaphore("s_" + name))
    with nc.Block() as block:
        @block.sync
        def _(e):
            P.emit("sq", e, sems)
            e.wait_ge(sems["oout"], counts["oout"])

        @block.gpsimd
        def _(e):
            P.emit("pool", e, sems)

        @block.tensor
        def _(e):
            P.emit("pe", e, sems)

        @block.scalar
        def _(e):
            P.emit("act", e, sems)

        @block.vector
        def _(e):
            P.emit("dve", e, sems)
    es.close()
    return nc, len(P.ops)


def make_in_maps(inputs, cores):
    cst = _consts()
    shared = {
        "norm_mix": np.ascontiguousarray(inputs["norm_mix"], dtype=np.float32),
        "norm_ffn": np.ascontiguousarray(inputs["norm_ffn"], dtype=np.float32),
        "norm_ple": np.ascontiguousarray(inputs["norm_ple"], dtype=np.float32),
        "norm_final": np.ascontiguousarray(inputs["norm_final"], dtype=np.float32).reshape(1, D),
        "ret_w_in": np.ascontiguousarray(inputs["ret_w_in"], dtype=np.float32),
        "ret_gn": np.ascontiguousarray(inputs["ret_gn"], dtype=np.float32).reshape(2, RH * RDV),
        "ret_w_out": np.ascontiguousarray(inputs["ret_w_out"], dtype=np.float32),
        "swa_w_in": np.ascontiguousarray(inputs["swa_w_in"], dtype=np.float32),
        "swa_sinks": np.ascontiguousarray(inputs["swa_sinks"], dtype=np.float32).reshape(1, 2 * SH),
        "swa_w_out": np.ascontiguousarray(inputs["swa_w_out"], dtype=np.float32),
        "ffn_w_in": np.ascontiguousarray(inputs["ffn_w_in"], dtype=np.float32),
        "ffn_w_out": np.ascontiguousarray(inputs["ffn_w_out"], dtype=np.float32),
        "ple_w_proj": np.ascontiguousarray(inputs["ple_w_proj"], dtype=np.float32),
        "ple_w_gate": np.ascontiguousarray(inputs["ple_w_gate"], dtype=np.float32),
    }
    shared.update(cst)
    x = np.asarray(inputs["x"])
    p = np.asarray(inputs["p"])
    maps = []
    for b in cores:
        m = dict(shared)
        m["x"] = np.ascontiguousarray(x[b], dtype=np.float32)
        m["p"] = np.ascontiguousarray(p[:, b], dtype=np.float32)
        maps.append(m)
    return maps


def kernel(**inputs):
    nc, _ = build()
    B = np.asarray(inputs["x"]).shape[0]
    in_maps = make_in_maps(inputs, list(range(B)))
    res = run_bass_kernel_spmd(nc, in_maps, core_ids=list(range(B)))
    out = np.stack([np.asarray(r["out"], dtype=np.float32) for r in res.results], axis=0)
    return out
```

```python
import math
from contextlib import ExitStack
from functools import partial

import numpy as np
import ml_dtypes

import concourse.bass as bass
import concourse.mybir as mybir
from concourse.bass_utils import run_bass_kernel_spmd

F32 = mybir.dt.float32
BF16 = mybir.dt.bfloat16
ALU = mybir.AluOpType
AF = mybir.ActivationFunctionType

D = 2048
SEQ = 4096
DEPTH = 4
TT = 512
NTB = TT // 128
DC = D // 128
RH, RDK, RDV = 8, 256, 512
RET_IN = 12288
RVW = 4096
SH, SKV, SHD = 32, 4, 64
SWA_IN = 2560
FH = 5632
PLE = 256
EPS = 1e-6
NWS = 3
WSLOT = 8192


class Op:
    __slots__ = ("eng", "fn", "deps", "sem", "inc", "need", "cnt", "isdma")


class Prog:
    def __init__(self):
        self.ops = []
        self.last_w = {}
        self.readers = {}

    def add(self, eng, fn, reads=(), writes=(), dma_sem=None):
        op = Op()
        op.eng = eng
        op.fn = fn
        op.isdma = dma_sem is not None
        op.sem = dma_sem if op.isdma else eng
        op.inc = 16 if op.isdma else 1
        op.need = op.isdma
        op.cnt = 0
        deps = []
        lw, rd = self.last_w, self.readers
        for r in reads:
            w = lw.get(r)
            if w is not None:
                deps.append(w)
        for r in writes:
            w = lw.get(r)
            if w is not None:
                deps.append(w)
            rs = rd.get(r)
            if rs:
                deps.extend(rs)
        for r in reads:
            rd.setdefault(r, []).append(op)
        for r in writes:
            lw[r] = op
            rd[r] = []
        dd = []
        seen = set()
        for p in deps:
            if id(p) in seen:
                continue
            seen.add(id(p))
            if (not p.isdma) and (not op.isdma) and p.eng == "pe" and eng == "pe":
                continue
            p.need = True
            dd.append(p)
        op.deps = dd
        self.ops.append(op)
        return op

    def finalize(self):
        counts = {}
        for op in self.ops:
            if op.need:
                counts[op.sem] = counts.get(op.sem, 0) + op.inc
                op.cnt = counts[op.sem]
        return counts

    def emit(self, eng_name, eng, sems):
        waited = {}
        for op in self.ops:
            if op.eng != eng_name:
                continue
            need = {}
            for p in op.deps:
                if p.cnt > need.get(p.sem, 0):
                    need[p.sem] = p.cnt
            for key, cnt in need.items():
                if waited.get(key, 0) >= cnt:
                    continue
                waited[key] = cnt
                eng.wait_ge(sems[key], cnt)
            ins = op.fn(eng)
            if op.need:
                ins.then_inc(sems[op.sem], op.inc)


def _consts():
    bf = ml_dtypes.bfloat16
    c = {}
    c["c_ident"] = np.eye(128, dtype=np.float32).astype(bf)
    c["c_identf"] = np.eye(128, dtype=np.float32)
    c["c_ones"] = np.ones((128, 128), dtype=np.float32).astype(bf)
    m = np.arange(128)[:, None]
    q = np.arange(128)[None, :]
    c["c_causal"] = (q >= m).astype(np.float32).astype(bf)
    log_g = np.log1p(-np.exp2(-5.0 - np.arange(RH, dtype=np.float64)))
    pos = np.arange(128, dtype=np.float64)
    qdec = np.exp(log_g[:, None] * (pos[None, :] + 1.0))
    c["c_qdec"] = np.broadcast_to(qdec.reshape(1, RH * 128), (128, RH * 128)).astype(np.float32).copy()
    kdec = (RDK ** -0.5) * np.exp(log_g[None, :] * (127.0 - pos[:, None]))
    c["c_kdec"] = kdec.astype(np.float32).copy()
    slopes = np.exp2(-8.0 * (np.arange(SH, dtype=np.float64) + 1.0) / SH)
    s = np.arange(128)[:, None]
    qq = np.arange(128)[None, :]
    tbl = np.zeros((128, SH, 2, 128), dtype=np.float64)
    rel_prev = (qq - s + 128).astype(np.float64)
    rel_cur = (qq - s).astype(np.float64)
    for h in range(SH):
        tbl[:, h, 0, :] = np.where(qq < s, np.exp(-slopes[h] * rel_prev), 0.0)
        tbl[:, h, 1, :] = np.where(qq >= s, np.exp(-slopes[h] * rel_cur), 0.0)
    c["c_swatbl"] = tbl.reshape(128, SH * 2 * 128).astype(np.float32).astype(bf)
    return c


RET_GAMMA = [1.0 - 2.0 ** (-5.0 - h) for h in range(RH)]
RET_CHUNK_DECAY = [float(np.float32(np.exp(np.float64(128.0) * np.log1p(-np.exp2(-5.0 - h))))) for h in range(RH)]
RET_INV_CD = [float(np.exp(-np.float64(128.0) * np.log1p(-np.exp2(-5.0 - h)))) for h in range(RH)]


def build(NT=SEQ // TT, NL=DEPTH):
    nc = bass.Bass("TRN2", target_bir_lowering=False)
    P = Prog()
    es = ExitStack()

    def din(name, shape, dt=F32):
        return nc.dram_tensor(name, list(shape), dt, kind="ExternalInput").ap()

    x_d = din("x", [SEQ, D])
    p_d = din("p", [DEPTH, SEQ, PLE])
    nmix_d = din("norm_mix", [DEPTH, D])
    nffn_d = din("norm_ffn", [DEPTH, D])
    nple_d = din("norm_ple", [DEPTH, D])
    nfin_d = din("norm_final", [1, D])
    rwin_d = din("ret_w_in", [2, D, RET_IN])
    rgn_d = din("ret_gn", [2, RH * RDV])
    rwout_d = din("ret_w_out", [2, RVW, D])
    swin_d = din("swa_w_in", [2, D, SWA_IN])
    ssink_d = din("swa_sinks", [1, 2 * SH])
    swout_d = din("swa_w_out", [2, D, D])
    fwin_d = din("ffn_w_in", [DEPTH, D, 2 * FH])
    fwout_d = din("ffn_w_out", [DEPTH, FH, D])
    pproj_d = din("ple_w_proj", [DEPTH, PLE, D])
    pgate_d = din("ple_w_gate", [DEPTH, D, D])
    cid_d = din("c_ident", [128, 128], BF16)
    cidf_d = din("c_identf", [128, 128])
    cones_d = din("c_ones", [128, 128], BF16)
    ccaus_d = din("c_causal", [128, 128], BF16)
    cqdec_d = din("c_qdec", [128, RH * 128])
    ckdec_d = din("c_kdec", [128, RH])
    ctbl_d = din("c_swatbl", [128, SH * 2 * 128], BF16)
    out_d = nc.dram_tensor("out", [SEQ, D], F32, kind="ExternalOutput").ap()
    st_d = nc.dram_tensor("state_scratch", [2 * RH * 128, 2 * RDV], F32, kind="Internal").ap()
    WCAP = 1000000
    wbf_ds = [nc.dram_tensor("wbf_scratch%d" % i, [128, WCAP], BF16, kind="Internal").ap() for i in range(2)]
    uoff = []
    walloc = {"i": 0, "off": 0}

    def sb(name, shape, dt):
        return es.enter_context(nc.sbuf_tensor(name, list(shape), dt))

    xT = sb("xT", [128, DC, TT], F32)
    hT = sb("hT", [128, DC, TT], BF16)
    big = sb("big", [128, 32, TT], BF16)
    wsl = sb("wsl", [128, NWS, WSLOT], BF16)
    mixb = sb("mixb", [128, 23040], BF16)
    mixf = sb("mixf", [128, 2, 2 * RDV], F32)
    gains = sb("gains", [128, 13, DC], F32)
    ident = sb("ident", [128, 128], BF16)
    identf = sb("identf", [128, 128], F32)
    ones = sb("ones", [128, 128], BF16)
    causal = sb("causal", [128, 128], BF16)
    qdec = sb("qdec", [128, RH, 128], F32)
    kdec = sb("kdec", [128, RH], F32)
    esink = sb("esink", [128, 2 * SH], F32)
    sq = sb("sq", [128, 2, TT], BF16)
    rstd = sb("rstd", [128, TT], F32)
    ftmp = sb("ftmp", [128, 2, TT], F32)
    kcar = sb("kcar", [128, 2, SKV * 2 * 128], BF16)
    vcar = sb("vcar", [128, 2, SKV * 65], BF16)
    stat = sb("stat", [128, 4, 16], F32)
    sstat = sb("sstat", [128, 4, 8], F32)
    epsb = sb("epsb", [128, 8], F32)
    rstd_tm = sb("rstd_tm", [128, NTB], F32)
    kr = sb("kr", [128, RH, NTB], F32)
    ps = es.enter_context(nc.psum_tensor("ps", [128, 8, 512], F32))

    bigf = big[:].rearrange("p a b -> p (a b)").bitcast(F32)
    stage = bigf.rearrange("p (t d) -> p t d", t=NTB)

    off = [0]

    def carve(n, shape_str=None, **kw):
        a = mixb[:, off[0]:off[0] + n]
        off[0] += n
        if shape_str:
            a = a.rearrange(shape_str, **kw)
        return a

    r_qT = [carve(2 * TT, "p (c t) -> p c t", c=2) for _ in range(2)]
    r_kpp = [carve(NTB * RDK, "p (t d) -> p t d", t=NTB) for _ in range(2)]
    r_v = [carve(NTB * RDV, "p (t e) -> p t e", t=NTB) for _ in range(2)]
    r_sg = [carve(NTB * RDV, "p (t e) -> p t e", t=NTB) for _ in range(2)]
    r_kT = [carve(2 * 128, "p (c t) -> p c t", c=2) for _ in range(2)]
    r_sT = [carve(128) for _ in range(2)]
    r_Sbf = [carve(2 * RDV, "p (c e) -> p c e", c=2) for _ in range(2)]
    r_yn = [carve(RDV) for _ in range(2)]
    r_z = [carve(RDV) for _ in range(2)]
    r_gn = carve(RH * RDV, "p (h e) -> p h e", h=RH)
    ret_end = off[0]
    off[0] = 0
    s_tbl = carve(SH * 2 * 128, "p (h k q) -> p h k q", h=SH, k=2)
    s_kpad = carve(NTB * SKV * 2 * 128, "p (t k a d) -> p t k a d", t=NTB, k=SKV, a=2)
    s_kTp = carve(5 * SKV * 2 * 128, "p (t k a s) -> p t k a s", t=5, k=SKV, a=2)
    s_vaug = carve(5 * SKV * 65, "p (t k e) -> p t k e", t=5, k=SKV)
    s_e = [carve(512) for _ in range(4)]
    s_otm = [carve(D) for _ in range(1)]
    swa_end = off[0]
    assert max(ret_end, swa_end) <= 23040, (ret_end, swa_end)

    ctr = {"bank": 0, "ws": 0, "sq": 0, "ft": 0, "alt": 0, "st": 0, "sst": 0}

    def next_bank():
        b = ctr["bank"]
        ctr["bank"] = (b + 1) % 6
        return b

    def alt_engine():
        ctr["alt"] ^= 1
        return "act" if ctr["alt"] else "dve"

    def MM(out, lhsT, rhs, start, stop, reads, writes):
        P.add("pe", lambda e: e.matmul(out, lhsT, rhs, start=start, stop=stop), reads, writes)

    def TR(out, in_, idt, reads, writes):
        P.add("pe", lambda e: e.transpose(out, in_, idt), reads, writes)

    def ACT(out, in_, func, reads, writes, bias=None, scale=None):
        kw = {}
        if bias is not None:
            kw["bias"] = bias
        if scale is not None:
            kw["scale"] = scale
        P.add("act", lambda e: e.activation(out, in_, func, **kw), reads, writes)

    def TT_(eng, out, in0, in1, op, reads, writes):
        P.add(eng, lambda e: e.tensor_tensor(out, in0, in1, op), reads, writes)

    def TS(eng, out, in0, s1, s2, op0, op1, reads, writes):
        if op1 is None:
            P.add(eng, lambda e: e.tensor_scalar(out, in0, s1, None, op0), reads, writes)
        else:
            P.add(eng, lambda e: e.tensor_scalar(out, in0, s1, s2, op0, op1), reads, writes)

    def STT(eng, out, in0, scalar, in1, op0, op1, reads, writes):
        P.add(eng, lambda e: e.scalar_tensor_tensor(out, in0, scalar, in1, op0, op1), reads, writes)

    def COPY(eng, out, in_, reads, writes):
        if eng == "act":
            P.add("act", lambda e: e.copy(out, in_), reads, writes)
        else:
            P.add(eng, lambda e: e.tensor_copy(out, in_), reads, writes)

    def DMA(q, out, in_, sem, reads, writes, slow=False):
        if slow:
            P.add(q, lambda e: e.dma_start(out=out, in_=in_, allow_slow_non_contiguous=True), reads, writes, dma_sem=sem)
        else:
            P.add(q, lambda e: e.dma_start(out=out, in_=in_), reads, writes, dma_sem=sem)

    def evac_copy(out, in_, reads, writes):
        COPY(alt_engine(), out, in_, reads, writes)

    cur = {"t": 0, "u": 0, "L": 0}
    CVT_TILE = {0: 0, 1: 0, 2: 0, 3: 0}

    def load_w(parts, KC):
        s = ctr["ws"]
        ctr["ws"] = (s + 1) % NWS
        u = cur["u"]
        cur["u"] += 1
        ntot = sum(pp.shape[1] for pp in parts)
        nel = KC * ntot
        assert nel <= WSLOT, (KC, ntot)
        view = wsl[:, s, 0:nel].rearrange("p (k n) -> p k n", k=KC)
        if cur["t"] == 0:
            if walloc["off"] + nel > WCAP:
                walloc["i"] += 1
                walloc["off"] = 0
            assert walloc["i"] < len(wbf_ds)
            uoff.append((walloc["i"], walloc["off"]))
            walloc["off"] += nel
        wi, wo = uoff[u]
        scr = wbf_ds[wi][:, wo:wo + nel]
        cvt = CVT_TILE[cur["L"]]
        if cur["t"] <= cvt:
            o = 0
            for pp in parts:
                n = pp.shape[1]
                DMA("pool", view[:, :, o:o + n], pp.rearrange("(k p) n -> p k n", p=128), "w%d" % s,
                    reads=[], writes=[("w", s)])
                o += n
            if cur["t"] == cvt and NT > cvt + 1:
                DMA("sq", scr, wsl[:, s, 0:KC * ntot], "wb%d" % s, reads=[("w", s)], writes=[("wbf", u)])
        else:
            DMA("sq", wsl[:, s, 0:KC * ntot], scr, "v%d" % s, reads=[("wbf", u)], writes=[("w", s)])
        return view, ("w", s)

    def lin_fm(wview, wres, KC, nsl, act_fn, col0=0):
        b = next_bank()
        for kc in range(KC):
            a, ares = act_fn(kc)
            MM(ps[:, b, :], wview[:, kc, col0 + nsl * 128:col0 + (nsl + 1) * 128], a, kc == 0, kc == KC - 1,
               reads=[wres, ares], writes=[("ps", b)])
        return b

    def lin_fm_g(out, wview, wres, KC, nsl, act_fn, col0=0):
        b = next_bank()
        out.append(b)
        for kc in range(KC):
            a, ares = act_fn(kc)
            MM(ps[:, b, :], wview[:, kc, col0 + nsl * 128:col0 + (nsl + 1) * 128], a, kc == 0, kc == KC - 1,
               reads=[wres, ares], writes=[("ps", b)])
            if kc == KC // 2 - 1:
                yield

    def lin_tm_g(out, wview, wres, KC, tb, c0, n, act3, actres_fn):
        b = next_bank()
        out.append(b)
        for kc in range(KC):
            MM(ps[:, b, 0:n], act3[:, kc, tb * 128:(tb + 1) * 128], wview[:, kc, c0:c0 + n], kc == 0, kc == KC - 1,
               reads=[wres, actres_fn(kc)], writes=[("ps", b)])
            if kc == KC // 2 - 1:
                yield

    def lin_tm(wview, wres, KC, tb, c0, n, act3, actres_fn):
        b = next_bank()
        for kc in range(KC):
            MM(ps[:, b, 0:n], act3[:, kc, tb * 128:(tb + 1) * 128], wview[:, kc, c0:c0 + n], kc == 0, kc == KC - 1,
               reads=[wres, actres_fn(kc)], writes=[("ps", b)])
        return b

    def hres(kc):
        return ("h", kc)

    def h_act(kc):
        return hT[:, kc, :], ("h", kc)

    def rmsnorm(grow, final=False):
        b = next_bank()
        for c in range(DC):
            i = ctr["sq"]
            ctr["sq"] ^= 1
            ACT(sq[:, i, :], xT[:, c, :], AF.Square, reads=[("x", c)], writes=[("sq", i)])
            MM(ps[:, b, :], ones[:], sq[:, i, :], c == 0, c == DC - 1, reads=[("sq", i), ("const",)], writes=[("ps", b)])
        ACT(rstd[:], ps[:, b, :], AF.Sqrt, reads=[("ps", b), ("eps",)], writes=[("rstd",)], bias=epsb[:, 0:1], scale=1.0 / D)
        P.add("dve", lambda e: e.reciprocal(rstd[:], rstd[:]), [("rstd",)], [("rstd",)])
        for c in range(DC):
            if final:
                STT("dve", xT[:, c, :], xT[:, c, :], gains[:, grow, c:c + 1], rstd[:], ALU.mult, ALU.mult,
                    reads=[("x", c), ("rstd",), ("const",)], writes=[("x", c)])
            else:
                STT("dve", hT[:, c, :], xT[:, c, :], gains[:, grow, c:c + 1], rstd[:], ALU.mult, ALU.mult,
                    reads=[("x", c), ("rstd",), ("const",)], writes=[("h", c)])

    def resid_add(ds, b):
        TT_("dve", xT[:, ds, :], xT[:, ds, :], ps[:, b, :], ALU.add, reads=[("x", ds), ("ps", b)], writes=[("x", ds)])

    def rmsnorm_pre(grow, need_tm):
        b = next_bank()
        for c in range(DC):
            i = ctr["sq"]
            ctr["sq"] ^= 1
            ACT(sq[:, i, :], xT[:, c, :], AF.Square, reads=[("x", c)], writes=[("sq", i)])
            MM(ps[:, b, :], ones[:], sq[:, i, :], c == 0, c == DC - 1, reads=[("sq", i), ("const",)], writes=[("ps", b)])
            ACT(hT[:, c, :], xT[:, c, :], AF.Identity, reads=[("x", c), ("const",)], writes=[("h", c)],
                scale=gains[:, grow, c:c + 1])
        ACT(rstd[:], ps[:, b, :], AF.Sqrt, reads=[("ps", b), ("eps",)], writes=[("rstd",)], bias=epsb[:, 0:1], scale=1.0 / D)
        P.add("dve", lambda e: e.reciprocal(rstd[:], rstd[:]), [("rstd",)], [("rstd",)])
        if need_tm:
            b2 = next_bank()
            for tb in range(NTB):
                TR(ps[:, b2, tb * 128:(tb + 1) * 128], rstd[:, tb * 128:(tb + 1) * 128], identf[:],
                   reads=[("rstd",), ("const",)], writes=[("ps", b2)])
            COPY("dve", rstd_tm[:], ps[:, b2, :].rearrange("p (t c) -> p t c", t=NTB)[:, :, 0],
                 reads=[("ps", b2)], writes=[("rstdtm",)])

    def evac_scale(out, in_, col, reads, writes):
        if alt_engine() == "act":
            ACT(out, in_, AF.Identity, reads=reads, writes=writes, scale=col)
        else:
            TS("dve", out, in_, col, None, ALU.mult, None, reads=reads, writes=writes)

    def resid_add_scaled(ds, b):
        fi = ctr["sst"] % 2
        ctr["sst"] = (ctr["sst"] + 1) % 4
        TT_("dve", ftmp[:, fi, :], ps[:, b, :], rstd[:], ALU.mult, reads=[("ps", b), ("rstd",)], writes=[("ft", fi)])
        TT_("dve", xT[:, ds, :], xT[:, ds, :], ftmp[:, fi, :], ALU.add, reads=[("x", ds), ("ft", fi)], writes=[("x", ds)])

    def load_consts():
        cl = []

        def CD(out, in_, slow=False):
            r = ("cst", len(cl))
            cl.append(r)
            DMA("sq", out, in_, "const", [], [r], slow=slow)

        CD(ident[:], cid_d)
        CD(identf[:], cidf_d)
        CD(ones[:], cones_d)
        CD(causal[:], ccaus_d)
        CD(qdec[:].rearrange("p h c -> p (h c)"), cqdec_d)
        CD(kdec[:], ckdec_d)
        for i, g in enumerate((nmix_d, nffn_d, nple_d)):
            for l in range(DEPTH):
                CD(gains[:, i * 4 + l, :], g[l:l + 1, :].rearrange("o (c p) -> p (o c)", p=128), slow=True)
        CD(gains[:, 12, :], nfin_d.rearrange("o (c p) -> p (o c)", p=128), slow=True)
        CD(esink[:], ssink_d.partition_broadcast(128))
        P.add("dve", lambda e: e.memset(epsb[:], EPS), cl, [("eps",), ("const",)])
        ACT(esink[:], esink[:], AF.Exp, reads=[("const",)], writes=[("const2",)])

    def load_x(t):
        DMA("act", stage, x_d[t * TT:(t + 1) * TT, :].rearrange("(tb p) d -> p tb d", p=128), "xin",
            reads=[], writes=[("big", c) for c in range(32)])
        for c in range(DC):
            b = next_bank()
            for tb in range(NTB):
                TR(ps[:, b, tb * 128:(tb + 1) * 128], stage[:, tb, c * 128:(c + 1) * 128], identf[:],
                   reads=[("big", tb * 8 + c // 2), ("const",)], writes=[("ps", b)])
            evac_copy(xT[:, c, :], ps[:, b, :], reads=[("ps", b)], writes=[("x", c)])

    def store_out(t):
        for tb in range(NTB):
            for cg in range(4):
                b = next_bank()
                for ci in range(4):
                    c = cg * 4 + ci
                    TR(ps[:, b, ci * 128:(ci + 1) * 128], xT[:, c, tb * 128:(tb + 1) * 128], identf[:],
                       reads=[("x", c), ("const",)], writes=[("ps", b)])
                evac_copy(stage[:, tb, cg * 512:(cg + 1) * 512], ps[:, b, :], reads=[("ps", b)],
                          writes=[("big", tb * 8 + cg * 2), ("big", tb * 8 + cg * 2 + 1)])
            DMA("act", out_d[t * TT + tb * 128:t * TT + (tb + 1) * 128, :], stage[:, tb, :], "oout",
                reads=[("big", tb * 8 + i) for i in range(8)], writes=[("outd",)])

    def ret_proj(t, L, h):
        j = L // 2
        W = rwin_d[j]
        hb = h % 2
        qT, kpp, v, sg = r_qT[hb], r_kpp[hb], r_v[hb], r_sg[hb]
        S = mixf[:, hb, :].rearrange("p (c e) -> p c e", c=2)
        srow = (j * RH + h) * 128
        if t == 0:
            P.add("dve", lambda e, S=S: e.memset(S, 0.0), [], [("S", hb)])
        else:
            DMA("act", S.rearrange("p c e -> p (c e)"), st_d[srow:srow + 128, :], "sld%d" % hb,
                reads=[("std", j, h)], writes=[("S", hb)])
        wv, wr = load_w([W[:, h * RDK:(h + 1) * RDK], W[:, D + h * RDK:D + (h + 1) * RDK]], DC)
        TT_("dve", ftmp[:, hb, :].rearrange("p (t c) -> p t c", t=NTB), rstd[:].rearrange("p (t c) -> p t c", t=NTB),
            qdec[:, h, :].unsqueeze(1).to_broadcast([128, NTB, 128]), ALU.mult,
            reads=[("rstd",), ("const",)], writes=[("ft", hb)])
        for dc in range(2):
            bl = []
            yield from lin_fm_g(bl, wv, wr, DC, dc, h_act)
            b = bl[0]
            TT_("dve", qT[:, dc, :], ps[:, b, :], ftmp[:, hb, :], ALU.mult,
                reads=[("ps", b), ("ft", hb)], writes=[("rq", hb)])
            yield
        for tb in range(NTB):
            bl = []
            yield from lin_tm_g(bl, wv, wr, DC, tb, 256, 256, hT, hres)
            b = bl[0]
            ACT(kpp[:, tb, :], ps[:, b, 0:256], AF.Identity, reads=[("ps", b), ("kr",)], writes=[("rk", hb)],
                scale=kr[:, h, tb:tb + 1])
            yield
        wv, wr = load_w([W[:, 2 * D + h * RDV:2 * D + (h + 1) * RDV]], DC)
        for tb in range(NTB):
            bl = []
            yield from lin_tm_g(bl, wv, wr, DC, tb, 0, 512, hT, hres)
            b = bl[0]
            evac_scale(v[:, tb, :], ps[:, b, :], rstd_tm[:, tb:tb + 1], reads=[("ps", b), ("rstdtm",)], writes=[("rv", hb)])
            yield
        wv, wr = load_w([W[:, 2 * D + RVW + h * RDV:2 * D + RVW + (h + 1) * RDV]], DC)
        for tb in range(NTB):
            bl = []
            yield from lin_tm_g(bl, wv, wr, DC, tb, 0, 512, hT, hres)
            b = bl[0]
            ACT(sg[:, tb, :], ps[:, b, :], AF.Silu, reads=[("ps", b), ("rstdtm",)], writes=[("rsg", hb)],
                scale=rstd_tm[:, tb:tb + 1])
            yield
        TT_("dve", sg, sg, r_gn[:, h, :].unsqueeze(1).to_broadcast([128, NTB, RDV]), ALU.mult,
            reads=[("rsg", hb), ("gn",)], writes=[("rsg", hb)])

    def ret_chunks(t, L, h):
        j = L // 2
        hb = h % 2
        qT, kpp, v, sg = r_qT[hb], r_kpp[hb], r_v[hb], r_sg[hb]
        S = mixf[:, hb, :].rearrange("p (c e) -> p c e", c=2)
        srow = (j * RH + h) * 128
        COPY("act", r_Sbf[0], S, reads=[("S", hb)], writes=[("Sbf", 0)])
        for cb in range(NTB):
            par = cb % 2
            kT = r_kT[par]
            Sb_cur = r_Sbf[cb % 2]
            Sb_nxt = r_Sbf[(cb + 1) % 2]
            csl = slice(cb * 128, (cb + 1) * 128)
            psb = ps[:, 6, :].bitcast(BF16)
            for dc in range(2):
                TR(psb[:, dc * 128:(dc + 1) * 128], kpp[:, cb, dc * 128:(dc + 1) * 128], ident[:],
                   reads=[("rk", hb), ("const",)], writes=[("ps", 6)])
            COPY("dve", kT.rearrange("p c t -> p (c t)"), psb[:, 0:256], reads=[("ps", 6)], writes=[("rkT", par)])
            bus = []
            for dc in range(2):
                bu = next_bank()
                bus.append(bu)
                MM(ps[:, bu, :], kpp[:, cb, dc * 128:(dc + 1) * 128], v[:, cb, :], True, True,
                   reads=[("rk", hb), ("rv", hb)], writes=[("ps", bu)])
            yield
            b = next_bank()
            for dc in range(2):
                MM(ps[:, b, 0:128], kT[:, dc, :], qT[:, dc, csl], dc == 0, dc == 1,
                   reads=[("rkT", par), ("rq", hb)], writes=[("ps", b)])
            sT = r_sT[par]
            STT("dve", sT, ps[:, b, 0:128], RET_INV_CD[h], causal[:], ALU.mult, ALU.mult,
                reads=[("ps", b), ("const",)], writes=[("rsT", par)])
            yield
            bo = next_bank()
            for dc in range(2):
                MM(ps[:, bo, :], qT[:, dc, csl], Sb_cur[:, dc, :], dc == 0, False,
                   reads=[("rq", hb), ("Sbf", cb % 2)], writes=[("ps", bo)])
            MM(ps[:, bo, :], sT, v[:, cb, :], False, True, reads=[("rsT", par), ("rv", hb)], writes=[("ps", bo)])
            for dc in range(2):
                STT("dve", S[:, dc, :], S[:, dc, :], RET_CHUNK_DECAY[h], ps[:, bus[dc], :], ALU.mult, ALU.add,
                    reads=[("S", hb), ("ps", bus[dc])], writes=[("S", hb)])
            if cb < NTB - 1:
                COPY("act", Sb_nxt, S, reads=[("S", hb)], writes=[("Sbf", (cb + 1) % 2)])
            yield
            si = ctr["st"]
            ctr["st"] = (si + 1) % 4
            st6 = stat[:, si, 0:6]
            mv = stat[:, si, 6:8]
            rs = stat[:, si, 8:9]
            nb = stat[:, si, 9:10]
            P.add("dve", lambda e, st6=st6, bo=bo: e.bn_stats(st6, ps[:, bo, :]), [("ps", bo)], [("stat", si)])
            P.add("dve", lambda e, st6=st6, mv=mv: e.bn_aggr(mv, st6), [("stat", si)], [("stat", si)])
            ACT(rs, mv[:, 1:2], AF.Sqrt, reads=[("stat", si), ("eps",)], writes=[("stat", si)], bias=epsb[:, 0:1], scale=1.0)
            P.add("dve", lambda e, rs=rs: e.reciprocal(rs, rs), [("stat", si)], [("stat", si)])
            STT("dve", nb, mv[:, 0:1], -1.0, rs, ALU.mult, ALU.mult, reads=[("stat", si)], writes=[("stat", si)])
            yn = r_yn[par]
            ACT(yn, ps[:, bo, :], AF.Identity, reads=[("ps", bo), ("stat", si)], writes=[("ryn", par)], bias=nb, scale=rs)
            z = r_z[par]
            TT_("dve", z, yn, sg[:, cb, :], ALU.mult, reads=[("ryn", par), ("rsg", hb)], writes=[("rz", par)])
            yield
            psb7 = ps[:, 7, :].bitcast(BF16)
            for ec in range(4):
                TR(psb7[:, ec * 128:(ec + 1) * 128], z[:, ec * 128:(ec + 1) * 128], ident[:],
                   reads=[("rz", par), ("const",)], writes=[("ps", 7)])
            COPY("act", big[:, h * 4:(h + 1) * 4, csl], psb7[:, 0:512].rearrange("p (e t) -> p e t", e=4),
                 reads=[("ps", 7)], writes=[("big", h * 4 + ec) for ec in range(4)])
            yield
        DMA("act", st_d[srow:srow + 128, :], S.rearrange("p c e -> p (c e)"), "sst%d" % hb,
            reads=[("S", hb)], writes=[("std", j, h)])

    def ret_layer(t, L):
        j = L // 2
        rmsnorm_pre(0 * 4 + L, True)
        TT_("dve", kr[:], rstd_tm[:].unsqueeze(1).to_broadcast([128, RH, NTB]), kdec[:].unsqueeze(2).to_broadcast([128, RH, NTB]),
            ALU.mult, reads=[("rstdtm",), ("const",)], writes=[("kr",)])
        alias = [("skTp", i) for i in range(5)] + [("svaug", i) for i in range(5)] + [("se", i) for i in range(4)] + [("sotm",), ("skpad",), ("stbl",)]
        DMA("pool", r_gn.rearrange("p h e -> p (h e)"), rgn_d[j:j + 1, :].partition_broadcast(128), "gn",
            reads=[], writes=[("gn",)] + alias)
        for _ in ret_proj(t, L, 0):
            pass
        for h in range(RH):
            A = ret_chunks(t, L, h)
            B = ret_proj(t, L, h + 1) if h + 1 < RH else None
            a_done = False
            b_done = B is None
            while not (a_done and b_done):
                if not a_done:
                    try:
                        next(A)
                    except StopIteration:
                        a_done = True
                if not b_done:
                    try:
                        next(B)
                    except StopIteration:
                        b_done = True
        Wo = rwout_d[j]
        for wt in range(8):
            wv, wr = load_w([Wo[:, wt * 256:(wt + 1) * 256]], 32)
            for dsl in range(2):
                b = lin_fm(wv, wr, 32, dsl, lambda kc: (big[:, kc, :], ("big", kc)))
                resid_add(wt * 2 + dsl, b)

    def swa_layer(t, L):
        j = L // 2
        W = swin_d[j]
        rmsnorm_pre(0 * 4 + L, True)
        DMA("act", s_tbl.rearrange("p h k q -> p (h k q)"), ctbl_d, "tbl", reads=[], writes=[("stbl",)])
        P.add("dve", lambda e: e.memset(s_kpad.rearrange("p t k a d -> p (t k a d)"), 0.0), [], [("skpad",)])
        P.add("dve", lambda e: e.memset(s_vaug.rearrange("p t k e -> p (t k e)"), 1.0), [], [("svaug", i) for i in range(5)])
        if t > 0:
            COPY("dve", s_kTp[:, 0].rearrange("p k a s -> p (k a s)"), kcar[:, j, :], reads=[("kcar", j)], writes=[("skTp", 0)])
            COPY("dve", s_vaug[:, 0].rearrange("p k e -> p (k e)"), vcar[:, j, :], reads=[("vcar", j)], writes=[("svaug", 0)])
        for wt in range(4):
            wv, wr = load_w([W[:, wt * 512:(wt + 1) * 512]], DC)
            for nsl in range(4):
                b = lin_fm(wv, wr, DC, nsl, h_act)
                TT_("dve", big[:, wt * 4 + nsl, :], ps[:, b, :], rstd[:], ALU.mult,
                    reads=[("ps", b), ("rstd",)], writes=[("big", wt * 4 + nsl)])
        wv, wr = load_w([W[:, D:D + 512]], DC)
        for tb in range(NTB):
            b = lin_tm(wv, wr, DC, tb, 0, 512, hT, hres)
            kin = ps[:, b, 0:256].rearrange("p (k d) -> p k d", k=SKV)
            rcol = rstd_tm[:, tb:tb + 1]
            ACT(s_kpad[:, tb, :, 0, 0:64], kin, AF.Identity, reads=[("ps", b), ("rstdtm",)], writes=[("skpad",)], scale=rcol)
            TS("dve", s_kpad[:, tb, :, 1, 64:128], kin, rcol, None, ALU.mult, None, reads=[("ps", b), ("rstdtm",)], writes=[("skpad",)])
            ACT(s_vaug[:, tb + 1, :, 0:64], ps[:, b, 256:512].rearrange("p (k d) -> p k d", k=SKV), AF.Identity,
                reads=[("ps", b), ("rstdtm",)], writes=[("svaug", tb + 1)], scale=rcol)
        for tb in range(NTB):
            bb = 6 + (tb % 2)
            psb = ps[:, bb, :].bitcast(BF16)
            for kh in range(SKV):
                for ab in range(2):
                    i = kh * 2 + ab
                    TR(psb[:, i * 128:(i + 1) * 128], s_kpad[:, tb, kh, ab, :], ident[:],
                       reads=[("skpad",), ("const",)], writes=[("ps", bb)])
            evac_copy(s_kTp[:, tb + 1].rearrange("p k a s -> p (k a s)"), psb[:, 0:1024], reads=[("ps", bb)], writes=[("skTp", tb + 1)])
        otm = s_otm[0]

        def unit_scores(tb, kh, hh):
            qsl = slice(tb * 128, (tb + 1) * 128)
            kts = [1] if (t == 0 and tb == 0) else [0, 1]
            h0 = kh * 8 + hh * 4
            etiles = []
            for kt in kts:
                slot = tb + kt
                b = next_bank()
                for i in range(4):
                    h = h0 + i
                    MM(ps[:, b, i * 128:(i + 1) * 128], s_kTp[:, slot, kh, h % 2, :], big[:, h // 2, qsl], True, True,
                       reads=[("skTp", slot), ("big", h // 2)], writes=[("ps", b)])
                ei = ctr["ft"]
                ctr["ft"] = (ei + 1) % 4
                e_ = s_e[ei]
                ACT(e_, ps[:, b, :], AF.Exp, reads=[("ps", b)], writes=[("se", ei)], scale=SHD ** -0.5)
                TT_("dve", e_.rearrange("p (h q) -> p h q", h=4), e_.rearrange("p (h q) -> p h q", h=4),
                    s_tbl[:, h0:h0 + 4, kt, :], ALU.mult, reads=[("se", ei), ("stbl",)], writes=[("se", ei)])
                etiles.append((e_, ei, slot))
            return etiles

        def unit_pv(tb, kh, hh, etiles):
            h0 = kh * 8 + hh * 4
            bo = next_bank()
            for i in range(4):
                for n, (e_, ei, slot) in enumerate(etiles):
                    MM(ps[:, bo, i * 65:(i + 1) * 65], e_[:, i * 128:(i + 1) * 128], s_vaug[:, slot, kh, :],
                       n == 0, n == len(etiles) - 1, reads=[("se", ei), ("svaug", slot)], writes=[("ps", bo)])
            si = ctr["sst"]
            ctr["sst"] = (si + 1) % 4
            den = sstat[:, si, 0:4]
            pv = ps[:, bo, 0:260].rearrange("p (h e) -> p h e", e=65)
            TT_("dve", den, pv[:, :, 64], esink[:, j * SH + h0:j * SH + h0 + 4], ALU.add,
                reads=[("ps", bo), ("const2",)], writes=[("sstat", si)])
            P.add("dve", lambda e, den=den: e.reciprocal(den, den), [("sstat", si)], [("sstat", si)])
            TT_("dve", otm[:, h0 * 64:(h0 + 4) * 64].rearrange("p (h d) -> p h d", h=4), pv[:, :, 0:64],
                den.unsqueeze(2).to_broadcast([128, 4, 64]), ALU.mult,
                reads=[("ps", bo), ("sstat", si)], writes=[("sotm",)])

        def tb_finish(tb):
            qsl = slice(tb * 128, (tb + 1) * 128)
            for g in range(2):
                bb = 6 + g
                psb = ps[:, bb, :].bitcast(BF16)
                for ci in range(8):
                    c = g * 8 + ci
                    TR(psb[:, ci * 128:(ci + 1) * 128], otm[:, c * 128:(c + 1) * 128], ident[:],
                       reads=[("sotm",), ("const",)], writes=[("ps", bb)])
                evac_copy(big[:, 16 + g * 8:16 + (g + 1) * 8, qsl], psb[:, 0:1024].rearrange("p (c q) -> p c q", c=8),
                          reads=[("ps", bb)], writes=[("big", 16 + g * 8 + ci) for ci in range(8)])

        units = [(tb, kh, hh) for tb in range(NTB) for kh in range(SKV) for hh in range(2)]
        pend = unit_scores(*units[0])
        for ui, u in enumerate(units):
            nxt = unit_scores(*units[ui + 1]) if ui + 1 < len(units) else None
            unit_pv(*u, pend)
            if u[1] == SKV - 1 and u[2] == 1:
                tb_finish(u[0])
            pend = nxt
        COPY("dve", kcar[:, j, :], s_kTp[:, 4].rearrange("p k a s -> p (k a s)"), reads=[("skTp", 4)], writes=[("kcar", j)])
        COPY("dve", vcar[:, j, :], s_vaug[:, 4].rearrange("p k e -> p (k e)"), reads=[("svaug", 4)], writes=[("vcar", j)])
        Wo = swout_d[j]
        for wt in range(4):
            wv, wr = load_w([Wo[:, wt * 512:(wt + 1) * 512]], DC)
            for nsl in range(4):
                b = lin_fm(wv, wr, DC, nsl, lambda kc: (big[:, 16 + kc, :], ("big", 16 + kc)))
                resid_add(wt * 4 + nsl, b)

    def ffn_layer(t, L):
        W = fwin_d[L]
        Wo = fwout_d[L]
        rmsnorm_pre(1 * 4 + L, False)
        HALF = FH // 2
        for half in range(2):
            for jt in range(11):
                c0 = half * HALF + jt * 256
                wv, wr = load_w([W[:, c0:c0 + 256], W[:, FH + c0:FH + c0 + 256]], DC)
                for jj in range(2):
                    hc = jt * 2 + jj
                    bg = lin_fm(wv, wr, DC, jj, h_act)
                    bu = lin_fm(wv, wr, DC, jj, h_act, col0=256)
                    fi = ctr["sst"] % 2
                    ctr["sst"] = (ctr["sst"] + 1) % 4
                    TT_("dve", ftmp[:, fi, :], ps[:, bg, :], rstd[:], ALU.mult, reads=[("ps", bg), ("rstd",)], writes=[("ft", fi)])
                    ACT(ftmp[:, fi, :], ftmp[:, fi, :], AF.Silu, reads=[("ft", fi)], writes=[("ft", fi)])
                    TT_("dve", big[:, hc, :], ftmp[:, fi, :], ps[:, bu, :], ALU.mult,
                        reads=[("ft", fi), ("ps", bu)], writes=[("big", hc)])
            for wt in range(8):
                wv, wr = load_w([Wo[half * HALF:(half + 1) * HALF, wt * 256:(wt + 1) * 256]], 22)
                for dsl in range(2):
                    b = lin_fm(wv, wr, 22, dsl, lambda kc: (big[:, kc, :], ("big", kc)))
                    resid_add_scaled(wt * 2 + dsl, b)

    def ple_layer(t, L):
        rmsnorm_pre(2 * 4 + L, False)
        pst = bigf[:, 0:NTB * PLE].rearrange("p (t d) -> p t d", t=NTB)
        pbf = big[:].rearrange("p a b -> p (a b)")[:, 2048:2048 + NTB * PLE].rearrange("p (t d) -> p t d", t=NTB)
        pT = big[:, 6:8, :]
        DMA("act", pst, p_d[L, t * TT:(t + 1) * TT, :].rearrange("(tb p) d -> p tb d", p=128), "pin",
            reads=[], writes=[("big", c) for c in range(4)])
        COPY("dve", pbf, pst, reads=[("big", c) for c in range(4)], writes=[("big", 4), ("big", 5)])
        psb = ps[:, 6, :].bitcast(BF16)
        for tb in range(NTB):
            for kc in range(2):
                TR(psb[:, kc * 512 + tb * 128:kc * 512 + (tb + 1) * 128], pbf[:, tb, kc * 128:(kc + 1) * 128], ident[:],
                   reads=[("big", 4), ("big", 5), ("const",)], writes=[("ps", 6)])
        COPY("act", pT, psb[:, 0:1024].rearrange("p (c t) -> p c t", c=2), reads=[("ps", 6)], writes=[("big", 6), ("big", 7)])
        Wg = pgate_d[L]
        Wp = pproj_d[L]
        for wt in range(4):
            wv, wr = load_w([Wg[:, wt * 512:(wt + 1) * 512]], DC)
            wpv, wpr = load_w([Wp[:, wt * 512:(wt + 1) * 512]], 2)
            for nsl in range(4):
                ds = wt * 4 + nsl
                bg = lin_fm(wv, wr, DC, nsl, h_act)
                bp = lin_fm(wpv, wpr, 2, nsl, lambda kc: (pT[:, kc, :], ("big", 6 + kc)))
                fi = ctr["sst"] % 2
                ctr["sst"] = (ctr["sst"] + 1) % 4
                TT_("dve", ftmp[:, fi, :], ps[:, bg, :], rstd[:], ALU.mult, reads=[("ps", bg), ("rstd",)], writes=[("ft", fi)])
                ACT(ftmp[:, fi, :], ftmp[:, fi, :], AF.Sigmoid, reads=[("ft", fi)], writes=[("ft", fi)])
                TT_("dve", ftmp[:, fi, :], ftmp[:, fi, :], ps[:, bp, :], ALU.mult, reads=[("ft", fi), ("ps", bp)], writes=[("ft", fi)])
                TT_("dve", xT[:, ds, :], xT[:, ds, :], ftmp[:, fi, :], ALU.add, reads=[("x", ds), ("ft", fi)], writes=[("x", ds)])

    load_consts()
    for t in range(NT):
        cur["t"] = t
        cur["u"] = 0
        load_x(t)
        for L in range(NL):
            cur["L"] = L
            if L % 2 == 0:
                ret_layer(t, L)
            else:
                swa_layer(t, L)
            ffn_layer(t, L)
            ple_layer(t, L)
        rmsnorm(12, final=True)
        store_out(t)
    counts = P.finalize()
    sems = {}
    for name in counts:
        sems[name] = es.enter_context(nc.semaphore("s_" + name))
    with nc.Block() as block:
        @block.sync
        def _(e):
            P.emit("sq", e, sems)
            e.wait_ge(sems["oout"], counts["oout"])

        @block.gpsimd
        def _(e):
            P.emit("pool", e, sems)

        @block.tensor
        def _(e):
            P.emit("pe", e, sems)

        @block.scalar
        def _(e):
            P.emit("act", e, sems)

        @block.vector
        def _(e):
            P.emit("dve", e, sems)
    es.close()
    return nc, len(P.ops)


def make_in_maps(inputs, cores):
    cst = _consts()
    shared = {
        "norm_mix": np.ascontiguousarray(inputs["norm_mix"], dtype=np.float32),
        "norm_ffn": np.ascontiguousarray(inputs["norm_ffn"], dtype=np.float32),
        "norm_ple": np.ascontiguousarray(inputs["norm_ple"], dtype=np.float32),
        "norm_final": np.ascontiguousarray(inputs["norm_final"], dtype=np.float32).reshape(1, D),
        "ret_w_in": np.ascontiguousarray(inputs["ret_w_in"], dtype=np.float32),
        "ret_gn": np.ascontiguousarray(inputs["ret_gn"], dtype=np.float32).reshape(2, RH * RDV),
        "ret_w_out": np.ascontiguousarray(inputs["ret_w_out"], dtype=np.float32),
        "swa_w_in": np.ascontiguousarray(inputs["swa_w_in"], dtype=np.float32),
        "swa_sinks": np.ascontiguousarray(inputs["swa_sinks"], dtype=np.float32).reshape(1, 2 * SH),
        "swa_w_out": np.ascontiguousarray(inputs["swa_w_out"], dtype=np.float32),
        "ffn_w_in": np.ascontiguousarray(inputs["ffn_w_in"], dtype=np.float32),
        "ffn_w_out": np.ascontiguousarray(inputs["ffn_w_out"], dtype=np.float32),
        "ple_w_proj": np.ascontiguousarray(inputs["ple_w_proj"], dtype=np.float32),
        "ple_w_gate": np.ascontiguousarray(inputs["ple_w_gate"], dtype=np.float32),
    }
    shared.update(cst)
    x = np.asarray(inputs["x"])
    p = np.asarray(inputs["p"])
    maps = []
    for b in cores:
        m = dict(shared)
        m["x"] = np.ascontiguousarray(x[b], dtype=np.float32)
        m["p"] = np.ascontiguousarray(p[:, b], dtype=np.float32)
        maps.append(m)
    return maps


def kernel(**inputs):
    nc, _ = build()
    B = np.asarray(inputs["x"]).shape[0]
    in_maps = make_in_maps(inputs, list(range(B)))
    res = run_bass_kernel_spmd(nc, in_maps, core_ids=list(range(B)))
    out = np.stack([np.asarray(r["out"], dtype=np.float32) for r in res.results], axis=0)
    return out
```
